# Optimizing a Trainium2 kernel written in Bass

```python
import jax, jax.numpy as jnp
from jax import lax
import numpy as np

D_MODEL = 2048
BATCH = 4
SEQ = 4096
DEPTH = 1

HEAD_DIM = 128
ATTN_WIDTH = D_MODEL // 2
ATTN_HEADS = ATTN_WIDTH // HEAD_DIM
CONV_WIDTH = D_MODEL - ATTN_WIDTH
CONV_GROUPS = CONV_WIDTH // HEAD_DIM
MIX_WIDTH = ATTN_WIDTH + CONV_WIDTH
IN_WIDTH = 3 * ATTN_WIDTH + 2 * CONV_WIDTH
DILATED_PATTERNS = ((128, 1), (512, 4), (2048, 16))
CONV_KERNEL = 31
D_FF = 5632
ROPE_THETA = 10000.0
NORM_EPS = 1e-6
N_MOD = 9
MASK_VALUE = -1e30

kernel_name = "hybrid_dilated_attn_conformer_conv_macaron"


def rms_norm(x, g):
    xf = x.astype(jnp.float32)
    y = xf * lax.rsqrt(jnp.mean(xf * xf, axis=-1, keepdims=True) + NORM_EPS)
    return (y * g.astype(jnp.float32)).astype(x.dtype)


def layer_norm(x, g, b):
    xf = x.astype(jnp.float32)
    mu = jnp.mean(xf, axis=-1, keepdims=True)
    xc = xf - mu
    y = xc * lax.rsqrt(jnp.mean(xc * xc, axis=-1, keepdims=True) + NORM_EPS)
    return (y * g.astype(jnp.float32) + b.astype(jnp.float32)).astype(x.dtype)


def modulate(h, shift, scale):
    return h * (1.0 + scale[:, None, :]) + shift[:, None, :]


def rope(t, pos):
    half = HEAD_DIM // 2
    inv_freq = ROPE_THETA ** (-jnp.arange(half, dtype=jnp.float32) / half)
    ang = pos.astype(jnp.float32)[:, None] * inv_freq[None, :]
    cos = jnp.cos(ang)[None, :, None, :]
    sin = jnp.sin(ang)[None, :, None, :]
    tf = t.astype(jnp.float32)
    t1, t2 = tf[..., :half], tf[..., half:]
    return jnp.concatenate([t1 * cos - t2 * sin, t2 * cos + t1 * sin], axis=-1).astype(t.dtype)


def dilated_window_attention(q, k, v, dilation, n_side):
    B, S, H, Dh = q.shape
    L = S // dilation
    blk = n_side
    nb = -(-L // blk)
    Lp = nb * blk

    def to_classes(t):
        return t.reshape(B, L, dilation, H, Dh).transpose(0, 2, 3, 1, 4)

    qc = jnp.pad(to_classes(q), ((0, 0), (0, 0), (0, 0), (0, Lp - L), (0, 0)))
    qb = qc.reshape(B, dilation, H, nb, blk, Dh)

    def key_blocks(t):
        tp = jnp.pad(to_classes(t), ((0, 0), (0, 0), (0, 0), (blk, Lp - L + blk), (0, 0)))
        tb = tp.reshape(B, dilation, H, nb + 2, blk, Dh)
        return jnp.concatenate([tb[:, :, :, :-2], tb[:, :, :, 1:-1], tb[:, :, :, 2:]], axis=4)

    kb = key_blocks(k)
    vb = key_blocks(v)

    m_q = jnp.arange(nb)[:, None] * blk + jnp.arange(blk)[None, :]
    m_k = jnp.arange(nb)[:, None] * blk - blk + jnp.arange(3 * blk)[None, :]
    rel = m_k[:, None, :] - m_q[:, :, None]
    valid = (jnp.abs(rel) <= n_side) & (m_k[:, None, :] >= 0) & (m_k[:, None, :] < L)

    s = jnp.einsum('bdhnqe,bdhnke->bdhnqk', qb, kb,
                   preferred_element_type=jnp.float32) * (Dh ** -0.5)
    s = jnp.where(valid, s, MASK_VALUE)
    lse = jax.nn.logsumexp(s, axis=-1)
    p = jnp.exp(s - lse[..., None])
    o = jnp.einsum('bdhnqk,bdhnke->bdhnqe', p, vb.astype(jnp.float32))

    o = o.reshape(B, dilation, H, Lp, Dh)[:, :, :, :L]
    o = o.transpose(0, 3, 1, 2, 4).reshape(B, S, H, Dh)
    lse = lse.reshape(B, dilation, H, Lp)[..., :L]
    lse = lse.transpose(0, 3, 1, 2).reshape(B, S, H)
    return o, lse


def swiglu(h, w_gate, w_up, w_down):
    return (jax.nn.silu(h @ w_gate) * (h @ w_up)) @ w_down


def setup_inputs(seed: int = 0) -> dict:
    key = jax.random.key(seed)
    ks = jax.random.split(key, 32)
    f32 = jnp.float32

    def nrm(k, shape, fan_in):
        return jax.random.normal(k, shape, f32) * (fan_in ** -0.5)

    def gain(k, n):
        return 1.0 + 0.01 * jax.random.normal(k, (DEPTH, n), f32)

    def small(k, n):
        return 0.01 * jax.random.normal(k, (DEPTH, n), f32)

    return {
        "x": jax.random.normal(ks[0], (BATCH, SEQ, D_MODEL), f32),
        "c": jax.random.normal(ks[1], (BATCH, D_MODEL), f32),
        "w_ada": nrm(ks[2], (DEPTH, D_MODEL, N_MOD * D_MODEL), D_MODEL),
        "b_ada": small(ks[3], N_MOD * D_MODEL),
        "ffn1_pre_g": gain(ks[4], D_MODEL),
        "ffn1_w_gate": nrm(ks[5], (DEPTH, D_MODEL, D_FF), D_MODEL),
        "ffn1_w_up": nrm(ks[6], (DEPTH, D_MODEL, D_FF), D_MODEL),
        "ffn1_w_down": nrm(ks[7], (DEPTH, D_FF, D_MODEL), D_FF),
        "ffn1_post_g": gain(ks[8], D_MODEL),
        "mix_pre_g": gain(ks[9], D_MODEL),
        "w_in": nrm(ks[10], (DEPTH, D_MODEL, IN_WIDTH), D_MODEL),
        "conv_w": nrm(ks[11], (DEPTH, CONV_KERNEL, CONV_WIDTH), CONV_KERNEL),
        "conv_b": small(ks[12], CONV_WIDTH),
        "conv_ln_g": gain(ks[13], CONV_WIDTH),
        "conv_ln_b": small(ks[14], CONV_WIDTH),
        "attn_out_g": gain(ks[15], ATTN_WIDTH),
        "conv_out_g": gain(ks[16], CONV_WIDTH),
        "w_out": nrm(ks[17], (DEPTH, MIX_WIDTH, D_MODEL), MIX_WIDTH),
        "mix_post_g": gain(ks[18], D_MODEL),
        "ffn2_pre_g": gain(ks[19], D_MODEL),
        "ffn2_w_gate": nrm(ks[20], (DEPTH, D_MODEL, D_FF), D_MODEL),
        "ffn2_w_up": nrm(ks[21], (DEPTH, D_MODEL, D_FF), D_MODEL),
        "ffn2_w_down": nrm(ks[22], (DEPTH, D_FF, D_MODEL), D_FF),
        "ffn2_post_g": gain(ks[23], D_MODEL),
    }


def reference(x, c, w_ada, b_ada, ffn1_pre_g, ffn1_w_gate, ffn1_w_up, ffn1_w_down,
              ffn1_post_g, mix_pre_g, w_in, conv_w, conv_b, conv_ln_g, conv_ln_b,
              attn_out_g, conv_out_g, w_out, mix_post_g, ffn2_pre_g, ffn2_w_gate,
              ffn2_w_up, ffn2_w_down, ffn2_post_g):
    B, S, D = x.shape
    pos = jnp.arange(S, dtype=jnp.int32)
    c_act = jax.nn.silu(c)

    for l in range(DEPTH):
        mod = c_act @ w_ada[l] + b_ada[l]
        (sh1, sc1, g1, sh2, sc2, g2, sh3, sc3, g3) = jnp.split(mod, N_MOD, axis=-1)

        h = modulate(rms_norm(x, ffn1_pre_g[l]), sh1, sc1)
        f = swiglu(h, ffn1_w_gate[l], ffn1_w_up[l], ffn1_w_down[l])
        x = x + 0.5 * g1[:, None, :] * rms_norm(f, ffn1_post_g[l])

        h = modulate(rms_norm(x, mix_pre_g[l]), sh2, sc2)
        proj = h @ w_in[l]
        q, k, v, c_val, c_gate = jnp.split(
            proj, [ATTN_WIDTH, 2 * ATTN_WIDTH, 3 * ATTN_WIDTH, 3 * ATTN_WIDTH + CONV_WIDTH],
            axis=-1)
        q = rope(q.reshape(B, S, ATTN_HEADS, HEAD_DIM), pos)
        k = rope(k.reshape(B, S, ATTN_HEADS, HEAD_DIM), pos)
        v = v.reshape(B, S, ATTN_HEADS, HEAD_DIM)

        outs, lses = [], []
        for window, dilation in DILATED_PATTERNS:
            o_i, lse_i = dilated_window_attention(q, k, v, dilation, window // (2 * dilation))
            outs.append(o_i)
            lses.append(lse_i)
        wts = jax.nn.softmax(jnp.stack(lses, axis=0), axis=0)
        attn = jnp.einsum('pbsh,pbshe->bshe', wts, jnp.stack(outs, axis=0))
        attn = attn.reshape(B, S, ATTN_WIDTH).astype(x.dtype)

        u = c_val * jax.nn.sigmoid(c_gate)
        u = lax.conv_general_dilated(
            u, conv_w[l][:, None, :].astype(u.dtype), window_strides=(1,),
            padding=[((CONV_KERNEL - 1) // 2, (CONV_KERNEL - 1) // 2)],
            dimension_numbers=('NWC', 'WIO', 'NWC'),
            feature_group_count=CONV_WIDTH) + conv_b[l]
        u = jax.nn.silu(layer_norm(u, conv_ln_g[l], conv_ln_b[l]))

        merged = jnp.concatenate([rms_norm(attn, attn_out_g[l]),
                                  rms_norm(u, conv_out_g[l])], axis=-1) @ w_out[l]
        x = x + g2[:, None, :] * rms_norm(merged, mix_post_g[l])

        h = modulate(rms_norm(x, ffn2_pre_g[l]), sh3, sc3)
        f = swiglu(h, ffn2_w_gate[l], ffn2_w_up[l], ffn2_w_down[l])
        x = x + 0.5 * g3[:, None, :] * rms_norm(f, ffn2_post_g[l])

    return x
```

```python
import numpy as np
import ml_dtypes
from contextlib import ExitStack
import concourse.bass as bass
import concourse.mybir as mybir
from concourse.bass_utils import run_bass_kernel_spmd

F32 = mybir.dt.float32
BF16 = mybir.dt.bfloat16
ALU = mybir.AluOpType
AF = mybir.ActivationFunctionType
AX = mybir.AxisListType

D = 2048
KC = 16
FF = 5632
FC = 44
T = 512
NSUB = 4
TOK_ALL = 3072
TOK_OWN = 2048
NT_ALL = TOK_ALL // T
NT_OWN = TOK_OWN // T
EPS = 1e-6
UPAD = 16
UW = UPAD + 2560
HEADS = 8
SCALE = 128 ** -0.5


class Buf:
    __slots__ = ("name", "w", "r", "dsem", "dcnt", "const")

    def __init__(self, name):
        self.name = name
        self.w = None
        self.r = {}
        self.dsem = None
        self.dcnt = 0
        self.const = False


class Sched:
    ENG = ("pe", "act", "dve", "pool", "sp")

    def __init__(self, nc, stack):
        self.nc = nc
        self.stack = stack
        self.h = {"pe": nc.tensor, "act": nc.scalar, "dve": nc.vector,
                  "pool": nc.gpsimd, "sp": nc.sync}
        self.sem = {}
        self.cnt = {}
        self.semobj = {}
        for e in self.ENG:
            s = stack.enter_context(nc.semaphore("s_" + e))
            self.sem[e] = s
            self.cnt[e] = 0
            self.semobj[e] = s
        self.seen = {e: {} for e in self.ENG}
        self.nbuf = 0
        self.ndsem = 0
        self.dbufs = []
        self._bykey = {}

    def buf(self, name=None):
        self.nbuf += 1
        return Buf(name or ("b%d" % self.nbuf))

    def bufs(self, n, name="b"):
        return [self.buf("%s%d" % (name, i)) for i in range(n)]

    def _dsem(self, b):
        key = ("d", id(b))
        if b.dsem is None:
            self.ndsem += 1
            b.dsem = self.stack.enter_context(self.nc.semaphore("d%d" % self.ndsem))
            self.semobj[key] = b.dsem
            self.dbufs.append(b)
            self._bykey[key] = b
        return key

    def _wait(self, e, key, count):
        if self.seen[e].get(key, 0) >= count:
            return
        self.seen[e][key] = count
        self.h[e].wait_ge(self.semobj[key], count)

    def _deps(self, e, reads, writes, selfsync, skip_key=None):
        deps = {}

        def add(k, c):
            if isinstance(k, tuple):
                c = max(c, self._bykey[k].dcnt)
            if deps.get(k, 0) < c:
                deps[k] = c
        for b in reads:
            if b.w is not None:
                add(*b.w)
        for b in writes:
            if b.w is not None and b.w[0] != skip_key:
                add(*b.w)
            for k, c in b.r.items():
                add(k, c)
        for k, c in deps.items():
            if k == e:
                if not selfsync or e == "pe":
                    continue
            self._wait(e, k, c)

    def _record(self, rec, reads, writes):
        k, c = rec
        for b in reads:
            if not b.const:
                if b.r.get(k, 0) < c:
                    b.r[k] = c
        for b in writes:
            b.w = rec
            b.r = {}

    def op(self, e, fn, reads=(), writes=(), selfsync=True):
        self._deps(e, reads, writes, selfsync)
        ins = fn(self.h[e])
        self.cnt[e] += 1
        ins.then_inc(self.sem[e], 1)
        self._record((e, self.cnt[e]), reads, writes)
        return ins

    def dma(self, q, out, in_, reads=(), writes=(), sembuf=None, **kw):
        sb = sembuf or (writes[0] if writes else reads[0])
        key = self._dsem(sb)
        self._deps(q, reads, writes, False, skip_key=key)
        ins = self.h[q].dma_start(out=out, in_=in_, **kw)
        sb.dcnt += 16
        ins.then_inc(sb.dsem, 16)
        self._record((key, sb.dcnt), reads, writes)
        return ins

    def wait_all(self, e, bufs):
        self._deps(e, [], bufs, False)

    def barrier(self):
        for e in self.ENG:
            for k in self.ENG:
                if k != e and self.cnt[k] > 0:
                    self._wait(e, k, self.cnt[k])
            for b in self.dbufs:
                if b.dcnt > 0:
                    self._wait(e, ("d", id(b)), b.dcnt)


class Unit:
    __slots__ = ("tag", "parts", "cache", "ncols")

    def __init__(self, tag, parts, cache=None):
        self.tag = tag
        self.parts = parts
        self.cache = cache
        self.ncols = max(off + a * b for (off, a, b, _) in parts)


class WRing:
    def __init__(self, S, nc, stack, nslots, plan):
        self.S = S
        self.n = nslots
        self.slots = [stack.enter_context(nc.sbuf_tensor("wr%d" % i, [128, 8192], BF16))
                      for i in range(nslots)]
        self.bufs = S.bufs(nslots, "wr")
        self.plan = plan
        self.issued = 0
        self.cur = 0
        self.cache_d = None
        self.cache_rec = {}

    def next(self, tag, hold=0):
        k = self.cur
        assert self.plan[k].tag == tag, (k, self.plan[k].tag, tag)
        lim = min(len(self.plan), k - hold + self.n)
        S = self.S
        while self.issued < lim:
            j = self.issued
            slot = self.slots[j % self.n]
            b = self.bufs[j % self.n]
            u = self.plan[j]
            if j >= self.n:
                old = self.plan[j - self.n]
                if old.cache is not None and old.cache[0] == "store":
                    idx = old.cache[1]
                    S.dma("pool", self.cache_d[idx * 128:(idx + 1) * 128, 0:old.ncols], slot[:, 0:old.ncols],
                          reads=[b], sembuf=b)
                    self.cache_rec[idx] = (("d", id(b)), b.dcnt)
            if u.cache is not None and u.cache[0] == "load":
                idx = u.cache[1]
                key, cnt = self.cache_rec[idx]
                S._wait("pool", key, cnt)
                S.dma("pool", slot[:, 0:u.ncols], self.cache_d[idx * 128:(idx + 1) * 128, 0:u.ncols], writes=[b])
            else:
                for (off, a, bb, src) in u.parts:
                    dst = slot[:, off:off + a * bb].rearrange("p (a b) -> p a b", a=a)
                    S.dma("pool", dst, src, writes=[b])
            self.issued += 1
        self.cur += 1
        return self.slots[k % self.n], self.bufs[k % self.n]


FGROUPS = ((0, 16), (16, 32), (32, 44))


def ffn_units(wg_v, wu_v, wd_v):
    us = []
    for u in range(FC // 2):
        us.append(Unit("gu", [(0, KC, 256, wg_v[:, :, u * 256:(u + 1) * 256]),
                              (4096, KC, 256, wu_v[:, :, u * 256:(u + 1) * 256])]))
    for n in range(4):
        for (f0, f1) in FGROUPS:
            us.append(Unit("dn", [(0, f1 - f0, 512, wd_v[:, f0:f1, n * 512:(n + 1) * 512])]))
    return us


def build_nc(dbg=None):
    dbg = dbg or {}
    n_tiles = dbg.get("n_tiles", NT_ALL)
    stop_after = dbg.get("stop_after", "all")

    nc = bass.Bass("TRN2", target_bir_lowering=False)

    def din(name, shape, dt=F32):
        return nc.dram_tensor(name, shape, dt, kind="ExternalInput").ap()

    def dscr(name, shape, dt=F32):
        kind = "ExternalOutput" if dbg.get("expose") else "Internal"
        return nc.dram_tensor(name, shape, dt, kind=kind).ap()

    x_d = din("x", [TOK_ALL, D])
    ccol_d = din("c_col", [128, KC])
    wada_d = din("w_ada", [D, 9 * D])
    bada_d = din("b_ada", [1, 9 * D])
    pregc_d = din("pre_g_cols", [128, 3 * KC])
    postg_d = din("post_g_rows", [1, 3 * D])
    f1g_d = din("ffn1_w_gate", [D, FF])
    f1u_d = din("ffn1_w_up", [D, FF])
    f1d_d = din("ffn1_w_down", [FF, D])
    f2g_d = din("ffn2_w_gate", [D, FF])
    f2u_d = din("ffn2_w_up", [D, FF])
    f2d_d = din("ffn2_w_down", [FF, D])
    win_d = din("w_in", [D, 5120])
    wout_d = din("w_out", [D, D])
    convw_d = din("conv_w_cols", [128, 8 * 31])
    cpar_d = din("cpar_cols", [128, 5 * 8])
    cos_d = din("rope_cos", [128, TOK_ALL])
    sin_d = din("rope_sin", [128, TOK_ALL])
    ident_d = din("ident", [128, 128], BF16)
    maskb_d = din("maskb", [128, 2 * 256], BF16)
    out_d = nc.dram_tensor("out", [TOK_OWN, D], F32, kind="ExternalOutput").ap()

    x1_d = dscr("x1_s", [TOK_OWN, D])
    qT_d = dscr("qT_s", [HEADS * 128, TOK_OWN], BF16)
    KPAD = 1024
    kT_d = dscr("kT_s", [HEADS * 128, KPAD + TOK_ALL], BF16)
    v_d = dscr("v_s", [KPAD + TOK_ALL, 1024], BF16)
    merged_d = dscr("merged_s", [16 * 128, TOK_OWN], BF16)
    x2_d = dscr("x2_s", [TOK_OWN, D])
    u_d = dscr("u_s", [8 * 128, UW], BF16)
    g2row_d = dscr("g2row_s", [1, D])
    g3row_d = dscr("g3row_s", [1, D])

    wada_v = wada_d.rearrange("(kc p) f -> p kc f", p=128)
    win_v = win_d.rearrange("(kc p) f -> p kc f", p=128)
    wout_v = wout_d.rearrange("(kc p) f -> p kc f", p=128)
    f1 = (f1g_d.rearrange("(kc p) f -> p kc f", p=128), f1u_d.rearrange("(kc p) f -> p kc f", p=128),
          f1d_d.rearrange("(fc p) d -> p fc d", p=128))
    f2 = (f2g_d.rearrange("(kc p) f -> p kc f", p=128), f2u_d.rearrange("(kc p) f -> p kc f", p=128),
          f2d_d.rearrange("(fc p) d -> p fc d", p=128))

    def win_unit(j):
        return Unit("win", [(0, KC, 512, win_v[:, :, j * 512:(j + 1) * 512])])

    def conv_unit(j):
        return Unit("cv", [(0, KC, 256, win_v[:, :, 3072 + 256 * j:3072 + 256 * (j + 1)]),
                           (4096, KC, 256, win_v[:, :, 4096 + 256 * j:4096 + 256 * (j + 1)])])

    plan = []
    NV0 = 5
    NVA = 2
    for j in range(4 * NVA):
        plan.append(Unit("ada", [(0, KC, 512, wada_v[:, :, j * 512:(j + 1) * 512])]))
    N_INTER = 4 * (NV0 - NVA)
    ncache = [0]
    cmap = {}

    def cached(units, group, first):
        out = []
        for q, u in enumerate(units):
            key = (group, q)
            if first:
                cmap[key] = ncache[0]
                ncache[0] += 1
                u.cache = ("store", cmap[key])
            else:
                u.cache = ("load", cmap[key])
            out.append(u)
        return out

    use_cache = dbg.get("cache", False)
    if stop_after != "p0":
        for i in range(n_tiles):
            fu = ffn_units(*f1)
            wu = [win_unit(j) for j in range(6)]
            cu = [conv_unit(j) for j in range(4)]
            if use_cache:
                fu = cached(fu, "f1", i == 0)
                wu = cached(wu, "win", i == 0)
                cu = cached(cu, "cv", i == 0)
            if i == 0:
                fu2 = []
                for q, u_ in enumerate(fu):
                    fu2.append(u_)
                    if q < N_INTER:
                        jj = 4 * NVA + q
                        fu2.append(Unit("ada", [(0, KC, 512, wada_v[:, :, jj * 512:(jj + 1) * 512])]))
                fu = fu2
            plan += fu
            plan += (wu[4:6] + wu[0:4]) if i < NT_OWN else (wu[4:6] + wu[2:4])
            if i <= NT_OWN:
                plan += cu
        for j in range(4 * NV0, 36):
            plan.append(Unit("ada", [(0, KC, 512, wada_v[:, :, j * 512:(j + 1) * 512])]))
    if stop_after == "all":
        for i in range(NT_OWN):
            ou = [Unit("wout", [(0, KC, 512, wout_v[:, :, n * 512:(n + 1) * 512])]) for n in range(4)]
            fu = ffn_units(*f2)
            if use_cache:
                ou = cached(ou, "wo", i == 0)
                fu = cached(fu, "f2", i == 0)
            plan += ou
            plan += fu
    cache_d = dscr("wcache_s", [max(ncache[0], 1) * 128, 8192], BF16)

    with ExitStack() as st:
        S = Sched(nc, st)

        def sb(name, shape, dt=F32, stack=st):
            return stack.enter_context(nc.sbuf_tensor("s_" + name, shape, dt))

        pa = [st.enter_context(nc.psum_tensor("pa%d" % i, [128, 512], F32)) for i in range(8)]
        pb = S.bufs(8, "pb")
        rot = {"acc": 0, "tr": 0}

        def rot6():
            i = rot["acc"]
            rot["acc"] = (i + 1) % 6
            return i

        def rot2():
            i = 6 + rot["tr"]
            rot["tr"] = (rot["tr"] + 1) % 2
            return i

        ident = sb("ident", [128, 128], BF16)
        ones_f = sb("ones_f", [128, 128])
        modcols = sb("modcols", [128, 6 * KC])
        pregc = sb("pregc", [128, 3 * KC])
        gvecA = sb("gvecA", [128, D])
        b_ident, b_ones, b_mod, b_pregc, b_gA = S.bufs(5, "c")

        ring = WRing(S, nc, st, dbg.get("nslots", 4), plan)
        ring.cache_d = cache_d

        S.dma("sp", ident[:], ident_d[:, :], writes=[b_ident])
        S.dma("sp", pregc[:], pregc_d[:, :], writes=[b_pregc])
        S.op("dve", lambda e: e.memset(ones_f[:], 1.0), writes=[b_ones])

        ccol = sb("ccol", [128, KC], F32)
        cact = sb("cact", [128, KC], BF16)
        b_ccol, b_cact = S.bufs(2, "p0")
        S.dma("sp", ccol[:], ccol_d[:, :], writes=[b_ccol])
        S.op("act", lambda e: e.activation(cact[:], ccol[:], AF.Silu), reads=[b_ccol], writes=[b_cact])

        def mod_vectors(vs, stk, views=None, vbufs=None):
            tg = "m%d" % vs[0]
            if views is None:
                mrow = [sb("mrow%s%d" % (tg, i), [1, 512], F32, stk) for i in range(2)]
                brow = [sb("brow%s%d" % (tg, i), [1, 512], F32, stk) for i in range(2)]
                prow = [sb("prow%s%d" % (tg, i), [1, 512], F32, stk) for i in range(2)]
                grow = [sb("grow%s%d" % (tg, i), [1, 512], F32, stk) for i in range(2)]
                tmpc = sb("tmpc" + tg, [128, KC], F32, stk)
            else:
                rows, tmpc = views
                mrow, brow, prow, grow = rows[0:2], rows[2:4], rows[4:6], rows[6:8]
            b_mrow = S.bufs(2, "mrow")
            b_brow = S.bufs(2, "brow")
            b_prow = S.bufs(2, "prow")
            b_grow = S.bufs(2, "grow")
            b_tmpc = S.buf("tmpc")
            if vbufs is not None:
                vbufs.extend(b_mrow + b_brow + b_prow + b_grow + [b_tmpc])
            np_ = 0
            for v in vs:
                k = v // 3
                kind = v % 3
                bkc = (6 + v % 2) if kind in (0, 1) else None
                for j in range(4):
                    pi = np_ % 2
                    np_ += 1
                    c0 = v * D + j * 512
                    S.dma("sp", brow[pi][:], bada_d[0:1, c0:c0 + 512], writes=[b_brow[pi]])
                    slot, sbf = ring.next("ada")
                    wv = slot[:, :].rearrange("p (a b) -> p a b", a=KC)
                    bk = rot6()
                    for kc in range(KC):
                        S.op("pe", lambda e, kc=kc, bk=bk: e.matmul(
                            pa[bk][0:1, :], cact[:, kc:kc + 1], wv[:, kc, :],
                            start=(kc == 0), stop=False),
                            reads=[b_cact, sbf], writes=[pb[bk]])
                    S.op("pe", lambda e, bk=bk, pi=pi: e.matmul(
                        pa[bk][0:1, :], ones_f[0:1, 0:1], brow[pi][0:1, :], start=False, stop=True),
                        reads=[b_ones, b_brow[pi]], writes=[pb[bk]])
                    S.op("act", lambda e, bk=bk, pi=pi: e.copy(mrow[pi][0:1, :], pa[bk][0:1, :]),
                         reads=[pb[bk]], writes=[b_mrow[pi]])
                    if kind in (0, 1):
                        for q in range(4):
                            kc = j * 4 + q
                            S.op("pe", lambda e, kc=kc, q=q, pi=pi: e.matmul(
                                pa[bkc][:, kc:kc + 1], mrow[pi][0:1, q * 128:(q + 1) * 128], ones_f[0:1, 0:1],
                                start=True, stop=True), reads=[b_mrow[pi], b_ones], writes=[pb[bkc]])
                    else:
                        half = 0.5 if k in (0, 2) else 1.0
                        S.dma("sp", prow[pi][:], postg_d[0:1, k * D + j * 512:k * D + (j + 1) * 512],
                              writes=[b_prow[pi]])
                        S.op("pool", lambda e, pi=pi: e.tensor_tensor(
                            grow[pi][0:1, :], mrow[pi][0:1, :], prow[pi][0:1, :], ALU.mult),
                            reads=[b_mrow[pi], b_prow[pi]], writes=[b_grow[pi]])
                        S.op("pool", lambda e, pi=pi, half=half: e.tensor_scalar(
                            grow[pi][0:1, :], grow[pi][0:1, :], half, None, ALU.mult),
                            reads=[b_grow[pi]], writes=[b_grow[pi]])
                        if k == 0:
                            bk2 = rot6()
                            S.op("pe", lambda e, bk2=bk2, pi=pi: e.matmul(
                                pa[bk2][:, :], ones_f[0:1, :], grow[pi][0:1, :],
                                start=True, stop=True), reads=[b_grow[pi], b_ones], writes=[pb[bk2]])
                            S.op("act", lambda e, j=j, bk2=bk2: e.copy(
                                gvecA[:, j * 512:(j + 1) * 512], pa[bk2][:, :]),
                                reads=[pb[bk2]], writes=[b_gA])
                        else:
                            S.dma("sp", (g2row_d if k == 1 else g3row_d)[0:1, j * 512:(j + 1) * 512],
                                  grow[pi][0:1, :], reads=[b_grow[pi]])
                    if j < 3:
                        yield
                if kind == 0:
                    S.op("act", lambda e, k=k: e.copy(
                        modcols[:, (2 * k + 1) * KC:(2 * k + 2) * KC], pa[bkc][:, 0:KC]),
                        reads=[pb[bkc]], writes=[b_mod])
                elif kind == 1:
                    S.op("act", lambda e, k=k: e.activation(
                        tmpc[:, :], pa[bkc][:, 0:KC], AF.Identity, bias=1.0, scale=1.0),
                        reads=[pb[bkc]], writes=[b_tmpc])
                    S.op("pool", lambda e, k=k: e.tensor_tensor(
                        modcols[:, (2 * k) * KC:(2 * k + 1) * KC], tmpc[:, :],
                        pregc[:, k * KC:(k + 1) * KC], ALU.mult),
                        reads=[b_tmpc, b_pregc], writes=[b_mod])
                yield

        with ExitStack() as p0:
            for _ in mod_vectors(list(range(NVA)), p0):
                pass
            S.barrier()
        b_ident.const = True
        b_ones.const = True

        if stop_after == "p0":
            dbg_mod = nc.dram_tensor("dbg_mod", [128, 6 * KC], F32, kind="ExternalOutput").ap()
            dbg_g = nc.dram_tensor("dbg_g", [128, D], F32, kind="ExternalOutput").ap()
            S.dma("sp", dbg_mod[:, :], modcols[:], reads=[b_mod])
            S.dma("sp", dbg_g[:, :], gvecA[:], reads=[b_gA])
            S.barrier()
            return nc


        def mcol(k, which, kc):
            c0 = (2 * k + which) * KC + kc
            return modcols[:, c0:c0 + 1]

        def prenorm_A_g(xs_, xb_, s, W):
            S.op("dve", lambda e: e.memset(W["pss"][:, s:s + 1], 0.0), writes=[W["pssb"][s]])
            yield
            S.op("act", lambda e: e.activation(W["xn"][s % 2][:], xs_[:], AF.Square,
                                               accum_out=W["pss"][:, s:s + 1]),
                 reads=[xb_], writes=[W["pssb"][s], W["xnb"][s % 2]])
            yield
            S.op("act", lambda e: e.activation(W["pss"][:, s:s + 1], W["pss"][:, s:s + 1], AF.Sqrt,
                                               bias=EPS, scale=1.0 / D),
                 reads=[W["pssb"][s]], writes=[W["pssb"][s]])
            yield
            S.op("dve", lambda e: e.reciprocal(W["pss"][:, s:s + 1], W["pss"][:, s:s + 1]),
                 reads=[W["pssb"][s]], writes=[W["pssb"][s]])
            yield
            xn = W["xn"][s % 2]
            xnb = W["xnb"][s % 2]
            S.op("dve", lambda e: e.tensor_scalar(xn[:], xs_[:], W["pss"][:, s:s + 1], None, ALU.mult),
                 reads=[xb_, W["pssb"][s]], writes=[xnb])
            yield

        def prenorm_A(xs_, xb_, s, W):
            for _ in prenorm_A_g(xs_, xb_, s, W):
                pass

        def prenorm(xs_, xb_, k, s, hT, hTb, W):
            prenorm_A(xs_, xb_, s, W)
            prenorm_B(k, s, hT, hTb, W)

        def prenorm_B(k, s, hT, hTb, W):
            xn = W["xn"][s % 2]
            xnb = W["xnb"][s % 2]
            for g4 in range(4):
                bk = rot2()
                ptb = pa[bk][:].bitcast(BF16)
                for j in range(4):
                    kc = g4 * 4 + j
                    S.op("pe", lambda e, j=j, kc=kc: e.transpose(
                        ptb[:, j * 128:(j + 1) * 128], xn[:, kc * 128:(kc + 1) * 128], ident[:]),
                        reads=[xnb, b_ident], writes=[pb[bk]])
                for j in range(4):
                    kc = g4 * 4 + j
                    dst = hT[:, kc, s * 128:(s + 1) * 128]
                    if g4 % 2 == 0:
                        S.op("dve", lambda e, j=j, kc=kc, dst=dst: e.tensor_scalar(
                            dst, ptb[:, j * 128:(j + 1) * 128], mcol(k, 0, kc), mcol(k, 1, kc),
                            ALU.mult, ALU.add), reads=[pb[bk], b_mod], writes=[hTb[s]])
                    else:
                        S.op("act", lambda e, j=j, kc=kc, dst=dst: e.activation(
                            dst, ptb[:, j * 128:(j + 1) * 128], AF.Identity,
                            bias=mcol(k, 1, kc), scale=mcol(k, 0, kc)),
                            reads=[pb[bk], b_mod], writes=[hTb[s]])

        def ffn_gateup(hT, hTb, W, hook=None):
            for u in range(FC // 2):
                if hook is not None and u >= 1:
                    hook(u - 1)
                slot, sbf = ring.next("gu")
                wg = slot[:, 0:4096].rearrange("p (a b) -> p a b", a=KC)
                wu = slot[:, 4096:8192].rearrange("p (a b) -> p a b", a=KC)
                for mm in range(2):
                    m = 2 * u + mm
                    ig = rot6()
                    iu = rot6()
                    for kc in range(KC):
                        S.op("pe", lambda e, kc=kc: e.matmul(
                            pa[ig][:, :], wg[:, kc, mm * 128:(mm + 1) * 128], hT[:, kc, :],
                            start=(kc == 0), stop=(kc == KC - 1)),
                            reads=[sbf] + hTb, writes=[pb[ig]])
                    for kc in range(KC):
                        S.op("pe", lambda e, kc=kc: e.matmul(
                            pa[iu][:, :], wu[:, kc, mm * 128:(mm + 1) * 128], hT[:, kc, :],
                            start=(kc == 0), stop=(kc == KC - 1)),
                            reads=[sbf] + hTb, writes=[pb[iu]])
                    sg = W["sgt"][m % 2]
                    sgb = W["sgb"][m % 2]
                    S.op("act", lambda e: e.activation(sg[:], pa[ig][:, :], AF.Silu),
                         reads=[pb[ig]], writes=[sgb])
                    S.op("dve", lambda e: e.tensor_tensor(W["actT"][:, m, :], sg[:], pa[iu][:, :], ALU.mult),
                         reads=[sgb, pb[iu]], writes=[W["actb"][m]])

        def alloc_ffn_work(W, stk, tg):
            hT_t = sb("hT" + tg, [128, KC * T], BF16, stk)
            hT = hT_t[:, :].rearrange("p (a b) -> p a b", a=KC)
            hTb = S.bufs(NSUB, "hT")
            actT_t = sb("actT" + tg, [128, FC * T], BF16, stk)
            W["actT"] = actT_t[:, :].rearrange("p (a b) -> p a b", a=FC)
            W["actb"] = S.bufs(FC, "act")
            W["f_sb"] = [sb("f_sb%s%d" % (tg, i), [128, D], F32, stk) for i in range(NSUB)]
            W["fb"] = S.bufs(NSUB, "f")
            xst = [sb("xst%s%d" % (tg, i), [128, D], F32, stk) for i in range(2)]
            xstb = S.bufs(2, "xst")
            W["xn"] = [sb("xn%s%d" % (tg, i), [128, D], BF16, stk) for i in range(2)]
            W["xnb"] = S.bufs(2, "xn")
            W["sgt"] = [sb("sgt%s%d" % (tg, i), [128, T], F32, stk) for i in range(2)]
            W["sgb"] = S.bufs(2, "sg")
            W["pss"] = sb("pss" + tg, [128, 8], F32, stk)
            W["pssb"] = S.bufs(8, "pss")
            W["prs"] = sb("prs" + tg, [128, 8], F32, stk)
            W["prsb"] = S.bufs(8, "prs")
            return hT, hTb, xst, xstb

        def proj_down(W, lhs, lhsb, nch, units_tag, groups, after_sub=None):
            def evac(s, n, bk):
                if s % 2 == 0:
                    S.op("dve", lambda e: e.tensor_copy(
                        W["f_sb"][s][:, n * 512:(n + 1) * 512], pa[bk][:, :]),
                        reads=[pb[bk]], writes=[W["fb"][s]])
                else:
                    S.op("act", lambda e: e.copy(
                        W["f_sb"][s][:, n * 512:(n + 1) * 512], pa[bk][:, :]),
                        reads=[pb[bk]], writes=[W["fb"][s]])

            for n in range(4):
                banks = [rot6() for _ in range(NSUB)]
                if n < 3:
                    for (f0, f1) in groups:
                        slot, sbf = ring.next(units_tag)
                        wd = slot[:, 0:(f1 - f0) * 512].rearrange("p (a b) -> p a b", b=512)
                        for s in range(NSUB):
                            for fc in range(f0, f1):
                                S.op("pe", lambda e, s=s, fc=fc: e.matmul(
                                    pa[banks[s]][:, :], lhs[:, fc, s * 128:(s + 1) * 128], wd[:, fc - f0, :],
                                    start=(fc == 0), stop=(fc == nch - 1)),
                                    reads=[lhsb[fc], sbf], writes=[pb[banks[s]]])
                    for s in range(NSUB):
                        evac(s, n, banks[s])
                else:
                    wds = []
                    for gi, (f0, f1) in enumerate(groups):
                        slot, sbf = ring.next(units_tag, hold=gi)
                        wds.append((slot[:, 0:(f1 - f0) * 512].rearrange("p (a b) -> p a b", b=512), sbf, f0, f1))
                    for s in range(NSUB):
                        for (wd, sbf, f0, f1) in wds:
                            for fc in range(f0, f1):
                                S.op("pe", lambda e, s=s, fc=fc, wd=wd, f0=f0: e.matmul(
                                    pa[banks[s]][:, :], lhs[:, fc, s * 128:(s + 1) * 128], wd[:, fc - f0, :],
                                    start=(fc == 0), stop=(fc == nch - 1)),
                                    reads=[lhsb[fc], sbf], writes=[pb[banks[s]]])
                        evac(s, n, banks[s])
                        if after_sub is not None:
                            epiA, epiB, epiX = after_sub
                            if s == 1:
                                run_interleaved([epiA(0), epiA(1)])
                            elif s == 3:
                                epiB(0)
                                epiB(1)
                                run_interleaved([epiA(2), epiA(3)])
                                epiX(0)
                                epiX(1)
                                epiB(2)
                                epiB(3)
                                epiX(2)
                                epiX(3)

        def ffn_down(W, after_sub=None):
            proj_down(W, W["actT"], W["actb"], FC, "dn", FGROUPS, after_sub)

        def postnorm_residual_g(s, W, gvec, gvb, xs_, xb_):
            r = W["prs"][:, s:s + 1]
            rb = W["prsb"][s]
            S.op("dve", lambda e: e.memset(r, 0.0), writes=[rb])
            yield
            S.op("act", lambda e: e.activation(W["xn"][s % 2][:], W["f_sb"][s][:], AF.Square, accum_out=r),
                 reads=[W["fb"][s]], writes=[rb, W["xnb"][s % 2]])
            yield
            S.op("act", lambda e: e.activation(r, r, AF.Sqrt, bias=EPS, scale=1.0 / D), reads=[rb], writes=[rb])
            yield
            S.op("dve", lambda e: e.reciprocal(r, r), reads=[rb], writes=[rb])
            yield
            f = W["f_sb"][s]
            S.op("dve", lambda e: e.scalar_tensor_tensor(f[:], f[:], r, gvec[:], ALU.mult, ALU.mult),
                 reads=[W["fb"][s], rb, gvb], writes=[W["fb"][s]])
            yield
            S.op("dve", lambda e: e.tensor_tensor(xs_[:], xs_[:], f[:], ALU.add),
                 reads=[xb_, W["fb"][s]], writes=[xb_])
            yield

        def postnorm_residual(s, W, gvec, gvb, xs_, xb_):
            for _ in postnorm_residual_g(s, W, gvec, gvb, xs_, xb_):
                pass

        def run_interleaved(gens):
            gens = list(gens)
            while gens:
                for g in list(gens):
                    try:
                        next(g)
                    except StopIteration:
                        gens.remove(g)

        with ExitStack() as p1:
            W = {}
            print("sbuf remaining at P1 start", nc.sbuf_bytes_remaining)
            hT, hTb, xst, xstb = alloc_ffn_work(W, p1, "a")
            cosT = sb("cosT", [128, T], F32, p1)
            sinT = sb("sinT", [128, T], F32, p1)
            b_cos, b_sin = S.bufs(2, "cs")
            tmpA, tmpB = W["sgt"]
            b_tA, b_tB = W["sgb"]
            print("sbuf remaining before staging", nc.sbuf_bytes_remaining)
            qst = [sb("qst%d" % i, [128, T], BF16, p1) for i in range(2)]
            qstb = S.bufs(2, "qst")
            vst = [sb("vst%d" % i, [128, T], BF16, p1) for i in range(2)]
            vstb = S.bufs(2, "vst")
            ust = [sb("ust%d" % i, [128, T], BF16, p1) for i in range(2)]
            ustb = S.bufs(2, "ust")
            zt = sb("zt", [128, UPAD], BF16, p1)
            b_zt = S.buf("zt")
            S.op("dve", lambda e: e.memset(zt[:], 0.0), writes=[b_zt])
            for cg in range(8):
                S.dma("sp", u_d[cg * 128:(cg + 1) * 128, 0:UPAD], zt[:], reads=[b_zt])
            zb = W["xn"][0]
            S.op("dve", lambda e: e.memset(zb[:, 0:KPAD], 0.0), writes=[W["xnb"][0]])
            for hh in range(8):
                S.dma("sp", kT_d[hh * 128:(hh + 1) * 128, 0:KPAD], zb[:, 0:KPAD], reads=[W["xnb"][0]])
                S.dma("sp", v_d[hh * 128:(hh + 1) * 128, :], zb[:, 0:1024], reads=[W["xnb"][0]])

            cnt = {"q": 0, "v": 0, "u": 0}
            for i in range(n_tiles):
                r0 = i * T
                own = i < NT_OWN
                for s in range(NSUB):
                    if i == 0 or s >= 2:
                        xs_, xb_ = xst[s % 2], xstb[s % 2]
                        S.dma("sp", xs_[:], x_d[r0 + s * 128:r0 + (s + 1) * 128, :], writes=[xb_])
                        prenorm_A(xs_, xb_, s, W)
                    prenorm_B(0, s, hT, hTb, W)
                if i == 0:
                    f2, f3 = W["f_sb"][2], W["f_sb"][3]
                    rows = [f2[0:1, q * 512:(q + 1) * 512] for q in range(4)] + \
                           [f3[0:1, q * 512:(q + 1) * 512] for q in range(4)]
                    vb = []
                    modgen1 = mod_vectors(list(range(NVA, NV0)), p1,
                                          views=(rows, W["f_sb"][1][:, 0:KC]), vbufs=vb)
                    ffn_gateup(hT, hTb, W, hook=lambda q: next(modgen1, None) if q < N_INTER else None)
                    for _ in modgen1:
                        pass
                    S.op("dve", lambda e: e.memset(W["f_sb"][1][:, 0:1], 0.0),
                         writes=vb + [W["fb"][1], W["fb"][2], W["fb"][3]])
                else:
                    ffn_gateup(hT, hTb, W)

                def epi1(s, r0=r0, own=own):
                    xs_, xb_ = xst[s % 2], xstb[s % 2]
                    S.dma("sp", xs_[:], x_d[r0 + s * 128:r0 + (s + 1) * 128, :], writes=[xb_])
                    yield
                    yield from postnorm_residual_g(s, W, gvecA, b_gA, xs_, xb_)
                    if own:
                        S.dma("sp", x1_d[r0 + s * 128:r0 + (s + 1) * 128, :], xs_[:], reads=[xb_])
                    yield from prenorm_A_g(xs_, xb_, s, W)

                vunits = []

                def vproj(s, r0=r0):
                    if not vunits:
                        for hq in range(2):
                            slot, sbf = ring.next("win", hold=hq)
                            vunits.append((slot[:, :].rearrange("p (a b) -> p a b", a=KC), sbf))
                    for jv, (wvv, sbfv) in enumerate(vunits):
                        bk = rot6()
                        for kc in range(KC):
                            S.op("pe", lambda e, kc=kc: e.matmul(
                                pa[bk][:, :], hT[:, kc, s * 128:(s + 1) * 128], wvv[:, kc, :],
                                start=(kc == 0), stop=(kc == KC - 1)),
                                reads=[sbfv, hTb[s]], writes=[pb[bk]])
                        vi = cnt["v"] % 2
                        cnt["v"] += 1
                        S.op("act", lambda e: e.copy(vst[vi][:], pa[bk][:, :]), reads=[pb[bk]], writes=[vstb[vi]])
                        S.dma("sp", v_d[KPAD + r0 + s * 128:KPAD + r0 + (s + 1) * 128, jv * 512:(jv + 1) * 512],
                              vst[vi][:], reads=[vstb[vi]])

                ffn_down(W, (epi1, lambda s: prenorm_B(1, s, hT, hTb, W), vproj))
                if i + 1 < n_tiles:
                    for s in range(2):
                        xs_, xb_ = xst[s % 2], xstb[s % 2]
                        S.dma("sp", xs_[:], x_d[r0 + T + s * 128:r0 + T + (s + 1) * 128, :], writes=[xb_])
                        prenorm_A(xs_, xb_, s, W)
                if dbg.get("p1_stop") == "c2":
                    break
                S.dma("sp", cosT[:], cos_d[:, r0:r0 + T], writes=[b_cos])
                S.dma("sp", sinT[:], sin_d[:, r0:r0 + T], writes=[b_sin])
                for j in (range(0, 4) if own else range(2, 4)):
                    slot, sbf = ring.next("win")
                    wv = slot[:, :].rearrange("p (a b) -> p a b", a=KC)
                    if j < 4:
                        for hh in range(4):
                            head = (j % 2) * 4 + hh
                            bk = rot6()
                            for kc in range(KC):
                                S.op("pe", lambda e, kc=kc: e.matmul(
                                    pa[bk][:, :], wv[:, kc, hh * 128:(hh + 1) * 128], hT[:, kc, :],
                                    start=(kc == 0), stop=(kc == KC - 1)),
                                    reads=[sbf] + hTb, writes=[pb[bk]])
                            S.op("dve", lambda e: e.tensor_tensor(tmpA[:], pa[bk][:, :], cosT[:], ALU.mult),
                                 reads=[pb[bk], b_cos], writes=[b_tA])
                            S.op("dve", lambda e: e.tensor_tensor(tmpB[0:64, :], pa[bk][64:128, :], sinT[0:64, :], ALU.mult),
                                 reads=[pb[bk], b_sin], writes=[b_tB])
                            S.op("dve", lambda e: e.tensor_tensor(tmpB[64:128, :], pa[bk][0:64, :], sinT[64:128, :], ALU.mult),
                                 reads=[pb[bk], b_sin], writes=[b_tB])
                            qi = cnt["q"] % 2
                            cnt["q"] += 1
                            S.op("dve", lambda e: e.tensor_tensor(qst[qi][:], tmpA[:], tmpB[:], ALU.add),
                                 reads=[b_tA, b_tB], writes=[qstb[qi]])
                            if j < 2:
                                dst = qT_d[head * 128:(head + 1) * 128, r0:r0 + T]
                            else:
                                dst = kT_d[head * 128:(head + 1) * 128, KPAD + r0:KPAD + r0 + T]
                            S.dma("sp", dst, qst[qi][:], reads=[qstb[qi]])
                    else:
                        for s in range(NSUB):
                            bk = rot6()
                            for kc in range(KC):
                                S.op("pe", lambda e, kc=kc: e.matmul(
                                    pa[bk][:, :], hT[:, kc, s * 128:(s + 1) * 128], wv[:, kc, :],
                                    start=(kc == 0), stop=(kc == KC - 1)),
                                    reads=[sbf, hTb[s]], writes=[pb[bk]])
                            vi = cnt["v"] % 2
                            cnt["v"] += 1
                            S.op("act", lambda e: e.copy(vst[vi][:], pa[bk][:, :]), reads=[pb[bk]], writes=[vstb[vi]])
                            S.dma("sp", v_d[KPAD + r0 + s * 128:KPAD + r0 + (s + 1) * 128, (j - 4) * 512:(j - 3) * 512],
                                  vst[vi][:], reads=[vstb[vi]])
                if i <= NT_OWN:
                    nt_c = T if own else 128
                    for j in range(4):
                        slot, sbf = ring.next("cv")
                        wcv = slot[:, 0:4096].rearrange("p (a b) -> p a b", a=KC)
                        wcg = slot[:, 4096:8192].rearrange("p (a b) -> p a b", a=KC)
                        for c2 in range(2):
                            cg = 2 * j + c2
                            iv = rot6()
                            ig = rot6()
                            for kc in range(KC):
                                S.op("pe", lambda e, kc=kc: e.matmul(
                                    pa[iv][:, 0:nt_c], wcv[:, kc, c2 * 128:(c2 + 1) * 128], hT[:, kc, 0:nt_c],
                                    start=(kc == 0), stop=(kc == KC - 1)),
                                    reads=[sbf] + hTb, writes=[pb[iv]])
                            for kc in range(KC):
                                S.op("pe", lambda e, kc=kc: e.matmul(
                                    pa[ig][:, 0:nt_c], wcg[:, kc, c2 * 128:(c2 + 1) * 128], hT[:, kc, 0:nt_c],
                                    start=(kc == 0), stop=(kc == KC - 1)),
                                    reads=[sbf] + hTb, writes=[pb[ig]])
                            sg = W["sgt"][cg % 2]
                            sgb = W["sgb"][cg % 2]
                            S.op("act", lambda e: e.activation(sg[:, 0:nt_c], pa[ig][:, 0:nt_c], AF.Sigmoid),
                                 reads=[pb[ig]], writes=[sgb])
                            ui = cnt["u"] % 2
                            cnt["u"] += 1
                            S.op("dve", lambda e: e.tensor_tensor(ust[ui][:, 0:nt_c], sg[:, 0:nt_c], pa[iv][:, 0:nt_c], ALU.mult),
                                 reads=[sgb, pb[iv]], writes=[ustb[ui]])
                            S.dma("sp", u_d[cg * 128:(cg + 1) * 128, UPAD + r0:UPAD + r0 + nt_c], ust[ui][:, 0:nt_c],
                                  reads=[ustb[ui]])
            S.barrier()

        if stop_after == "p1":
            return nc

        def rms_merge(allv, allb, gcol, row_base, sq, sqb, rstd, b_rstd, mg, mgb):
            for g in range(8):
                S.op("act", lambda e, g=g: e.activation(sq[g % 2], allv[:, g, :], AF.Square),
                     reads=[allb[g]], writes=[sqb[g % 2]])
                for tb in range(4):
                    S.op("pe", lambda e, g=g, tb=tb: e.matmul(
                        pa[tb][:, :], ones_f[:, :], sq[g % 2][:, tb * 512:(tb + 1) * 512],
                        start=(g == 0), stop=(g == 7)), reads=[sqb[g % 2], b_ones], writes=[pb[tb]])
            for tb in range(4):
                S.op("act", lambda e, tb=tb: e.activation(
                    rstd[:, tb * 512:(tb + 1) * 512], pa[tb][:, :], AF.Sqrt, bias=EPS, scale=1.0 / 1024),
                    reads=[pb[tb]], writes=[b_rstd])
            S.op("dve", lambda e: e.reciprocal(rstd, rstd), reads=[b_rstd], writes=[b_rstd])
            for g in range(8):
                S.op("dve", lambda e, g=g: e.scalar_tensor_tensor(
                    mg[g % 2], allv[:, g, :], gcol(g), rstd, ALU.mult, ALU.mult),
                    reads=[allb[g], b_rstd], writes=[mgb[g % 2]])
                S.dma("sp", merged_d[(row_base + g) * 128:(row_base + g + 1) * 128, :], mg[g % 2],
                      reads=[mgb[g % 2]])

        cpar = sb("cpar", [128, 40])
        b_cpar = S.buf("cpar")
        S.dma("sp", cpar[:], cpar_d[:, :], writes=[b_cpar])

        with ExitStack() as p3:
            call = sb("call", [128, 8 * TOK_OWN], F32, p3)
            callv = call[:, :].rearrange("p (g t) -> p g t", g=8)
            callb = S.bufs(8, "call")
            utb16 = [sb("utb%d" % i, [128, TOK_OWN + 32], BF16, p3) for i in range(2)]
            utbb = S.bufs(2, "utb")
            ut0 = sb("ut0", [128, TOK_OWN], F32, p3)
            ut = [ut0, ut0]
            ut0b = S.buf("ut")
            utb = [ut0b, ut0b]
            dg = [sb("dg%d" % i, [128, 31 * 128], BF16, p3) for i in range(2)]
            dgb = S.bufs(2, "dg")
            cw = sb("cw", [128, 8 * 31], F32, p3)
            b_cw = S.buf("cw")
            S.dma("sp", cw[:], convw_d[:, :], writes=[b_cw])
            modgen = mod_vectors(list(range(NV0, 9)), p3)
            for cg in range(8):
                u_, ub_ = utb16[cg % 2], utbb[cg % 2]
                S.dma("sp", u_[:], u_d[cg * 128:(cg + 1) * 128, 0:TOK_OWN + 32], writes=[ub_])
                dg_, dgb_ = dg[cg % 2], dgb[cg % 2]
                for j in range(31):
                    wj = cw[:, cg * 31 + j:cg * 31 + j + 1]
                    S.op("dve", lambda e, wj=wj, j=j: e.tensor_scalar(
                        dg_[:, j * 128:(j + 1) * 128], ident[:, :], wj, None, ALU.mult),
                        reads=[b_ident, b_cw], writes=[dgb_], selfsync=False)
                for tb in range(4):
                    bk = rot6()
                    for j in range(31):
                        S.op("pe", lambda e, j=j, tb=tb: e.matmul(
                            pa[bk][:, :], dg_[:, j * 128:(j + 1) * 128],
                            u_[:, 1 + j + tb * 512:1 + j + (tb + 1) * 512],
                            start=(j == 0), stop=(j == 30)), reads=[dgb_, ub_], writes=[pb[bk]])
                    dst = callv[:, cg, tb * 512:(tb + 1) * 512]
                    if tb % 2 == 0:
                        S.op("act", lambda e, dst=dst: e.activation(
                            dst, pa[bk][:, :], AF.Identity, bias=cpar[:, cg:cg + 1], scale=1.0),
                            reads=[pb[bk], b_cpar], writes=[callb[cg]])
                    else:
                        S.op("dve", lambda e, dst=dst: e.tensor_scalar(
                            dst, pa[bk][:, :], cpar[:, cg:cg + 1], None, ALU.add),
                            reads=[pb[bk], b_cpar], writes=[callb[cg]])
                for _ in range(2):
                    next(modgen, None)
            for _ in modgen:
                pass
            sqa = [ut[i][:, :] for i in range(2)]
            sqab = utb
            mean = sb("mean", [128, TOK_OWN], F32, p3)
            b_mean = S.buf("mean")
            lrs = sb("lrs", [128, TOK_OWN], F32, p3)
            b_lrs = S.buf("lrs")
            for cg in range(8):
                for tb in range(4):
                    S.op("pe", lambda e, cg=cg, tb=tb: e.matmul(
                        pa[tb][:, :], ones_f[:, :], callv[:, cg, tb * 512:(tb + 1) * 512],
                        start=(cg == 0), stop=(cg == 7)), reads=[callb[cg], b_ones], writes=[pb[tb]])
            for tb in range(4):
                S.op("act", lambda e, tb=tb: e.activation(
                    mean[:, tb * 512:(tb + 1) * 512], pa[tb][:, :], AF.Copy, scale=1.0 / 1024),
                    reads=[pb[tb]], writes=[b_mean])
            for cg in range(8):
                S.op("act", lambda e, cg=cg: e.activation(sqa[cg % 2], callv[:, cg, :], AF.Square),
                     reads=[callb[cg]], writes=[sqab[cg % 2]])
                for tb in range(4):
                    S.op("pe", lambda e, cg=cg, tb=tb: e.matmul(
                        pa[tb][:, :], ones_f[:, :], sqa[cg % 2][:, tb * 512:(tb + 1) * 512],
                        start=(cg == 0), stop=(cg == 7)), reads=[sqab[cg % 2], b_ones], writes=[pb[tb]])
            S.op("dve", lambda e: e.tensor_tensor(lrs[:], mean[:], mean[:], ALU.mult),
                 reads=[b_mean], writes=[b_lrs])
            for tb in range(4):
                S.op("dve", lambda e, tb=tb: e.scalar_tensor_tensor(
                    lrs[:, tb * 512:(tb + 1) * 512], pa[tb][:, :], 1.0 / 1024, lrs[:, tb * 512:(tb + 1) * 512],
                    ALU.mult, ALU.subtract), reads=[pb[tb], b_lrs], writes=[b_lrs])
            S.op("act", lambda e: e.activation(lrs[:], lrs[:], AF.Sqrt, bias=EPS, scale=1.0),
                 reads=[b_lrs], writes=[b_lrs])
            S.op("dve", lambda e: e.reciprocal(lrs[:], lrs[:]), reads=[b_lrs], writes=[b_lrs])
            for cg in range(8):
                acc = callv[:, cg, :]
                S.op("dve", lambda e, acc=acc: e.tensor_tensor(acc, acc, mean[:], ALU.subtract),
                     reads=[b_mean], writes=[callb[cg]])
                S.op("dve", lambda e, acc=acc: e.tensor_tensor(acc, acc, lrs[:], ALU.mult),
                     reads=[b_lrs], writes=[callb[cg]])
                S.op("act", lambda e, acc=acc, cg=cg: e.activation(
                    acc, acc, AF.Silu, bias=cpar[:, 16 + cg:16 + cg + 1], scale=cpar[:, 8 + cg:8 + cg + 1]),
                    reads=[b_cpar], writes=[callb[cg]])
            rms_merge(callv, callb, lambda g: cpar[:, 24 + g:24 + g + 1], 8, sqa, sqab, lrs[:, :], b_lrs,
                      [dg[0][:, 0:TOK_OWN], dg[1][:, 0:TOK_OWN]], dgb)
            S.barrier()

        if stop_after == "p3":
            return nc

        with ExitStack() as p2:
            aall = sb("aall", [128, 8 * TOK_OWN], F32, p2)
            aallv = aall[:, :].rearrange("p (g t) -> p g t", g=8)
            aallb = S.bufs(8, "aall")
            qh = sb("qh", [128, TOK_OWN], BF16, p2)
            kh = sb("kh", [128, KPAD + TOK_ALL], BF16, p2)
            b_qh, b_kh = S.bufs(2, "qk")
            NVT = 69
            vt = [sb("vt%d" % i, [128, NVT * 128], BF16, p2) for i in range(2)]
            vtb = S.bufs(2, "vt")
            accn = sb("accn", [128, TOK_OWN], F32, p2)
            accd = sb("accd", [128, TOK_OWN], F32, p2)
            b_an, b_ad = S.bufs(2, "acc")
            ptl = [sb("pt%d" % i, [128, 256], BF16, p2) for i in range(3)]
            ptb_ = S.bufs(3, "pt")
            maskb = sb("maskb", [128, 512], BF16, p2)
            b_mask = S.buf("mask")
            ones_b = sb("ones_b", [128, 128], BF16, p2)
            b_onesb = S.buf("onesb")
            S.dma("sp", maskb[:], maskb_d[:, :], writes=[b_mask])
            S.op("dve", lambda e: e.memset(ones_b[:], 1.0), writes=[b_onesb])
            b_mask.const = True
            b_onesb.const = True
            PATS = ((1, 16), (4, 4), (16, 1))
            nchunk = 0
            vbase = {}

            def load_v(h):
                vth, vthb = vt[h % 2], vtb[h % 2]
                vidx = 0
                for (d, nblk) in PATS:
                    nch = nblk + 1
                    base = KPAD - 64 * d
                    rows = v_d[base:base + 128 * d * nch, h * 128:(h + 1) * 128].rearrange(
                        "(c a dd) e -> a c dd e", a=128, dd=d)
                    for r in range(d):
                        dst = vth[:, vidx * 128:(vidx + nch) * 128].rearrange("p (c e) -> p c e", c=nch)
                        S.dma("sp", dst, rows[:, :, r, :], writes=[vthb])
                        vbase[(d, r)] = vidx
                        vidx += nch

            load_v(0)
            for h in range(HEADS):
                S.dma("sp", qh[:], qT_d[h * 128:(h + 1) * 128, :], writes=[b_qh])
                S.dma("sp", kh[:], kT_d[h * 128:(h + 1) * 128, :], writes=[b_kh])
                if h + 1 < HEADS:
                    load_v(h + 1)
                vth, vthb = vt[h % 2], vtb[h % 2]
                chunks = []
                for pi, (d, nblk) in enumerate(PATS):
                    for r in range(d):
                        for c in range(nblk + 1):
                            chunks.append((pi, d, nblk, r, c))

                def emit_scores(ci):
                    pi, d, nblk, r, c = chunks[ci]
                    qv = qh[:, :].rearrange("p (m dd) -> p dd m", dd=d)
                    kv = kh[:, :].rearrange("p (m dd) -> p dd m", dd=d)
                    q0 = max(c - 1, 0) * 128
                    q1 = min(c + 1, nblk) * 128
                    ncol = q1 - q0
                    mc0 = (1 if c == 0 else 0) * 256 + (128 if c == 0 else 0)
                    kst = KPAD // d - 64 + 128 * c
                    g = nchunk0 + ci
                    bst = g % 2
                    pti = g % 3
                    S.op("pe", lambda e: e.matmul(
                        pa[bst][:, 0:ncol], kv[:, r, kst:kst + 128], qv[:, r, q0:q1],
                        start=True, stop=False), reads=[b_kh, b_qh], writes=[pb[bst]])
                    S.op("pe", lambda e: e.matmul(
                        pa[bst][:, 0:ncol], ident[:, :], maskb[:, mc0:mc0 + ncol],
                        start=False, stop=True), reads=[b_ident, b_mask], writes=[pb[bst]])
                    S.op("act", lambda e: e.activation(
                        ptl[pti][:, 0:ncol], pa[bst][:, 0:ncol], AF.Exp, scale=SCALE),
                        reads=[pb[bst]], writes=[ptb_[pti]])

                def emit_pv(ci):
                    pi, d, nblk, r, c = chunks[ci]
                    anv = accn[:, :].rearrange("p (m dd) -> p dd m", dd=d)
                    adv = accd[:, :].rearrange("p (m dd) -> p dd m", dd=d)
                    q0 = max(c - 1, 0) * 128
                    g = nchunk0 + ci
                    pti = g % 3
                    vtile = vth[:, (vbase[(d, r)] + c) * 128:(vbase[(d, r)] + c + 1) * 128]
                    for jb in (c - 1, c):
                        if jb < 0 or jb >= nblk:
                            continue
                        pc0 = jb * 128 - q0
                        first = (jb == c)
                        bn = 2 + jb % 2
                        bd = 4 + jb % 2
                        S.op("pe", lambda e: e.matmul(
                            pa[bn][:, 0:128], vtile, ptl[pti][:, pc0:pc0 + 128],
                            start=first, stop=not first), reads=[vthb, ptb_[pti]], writes=[pb[bn]])
                        S.op("pe", lambda e: e.matmul(
                            pa[bd][:, 0:128], ones_b[:, :], ptl[pti][:, pc0:pc0 + 128],
                            start=first, stop=not first), reads=[b_onesb, ptb_[pti]], writes=[pb[bd]])
                    if c >= 1:
                        jb = c - 1
                        bn = 2 + jb % 2
                        bd = 4 + jb % 2
                        dn_ = anv[:, r, jb * 128:(jb + 1) * 128]
                        dd_ = adv[:, r, jb * 128:(jb + 1) * 128]
                        if pi == 0:
                            S.op("act", lambda e: e.copy(dn_, pa[bn][:, 0:128]),
                                 reads=[pb[bn]], writes=[b_an], selfsync=False)
                            S.op("act", lambda e: e.copy(dd_, pa[bd][:, 0:128]),
                                 reads=[pb[bd]], writes=[b_ad], selfsync=False)
                        else:
                            S.op("dve", lambda e: e.tensor_tensor(dn_, dn_, pa[bn][:, 0:128], ALU.add),
                                 reads=[pb[bn]], writes=[b_an], selfsync=False)
                            S.op("dve", lambda e: e.tensor_tensor(dd_, dd_, pa[bd][:, 0:128], ALU.add),
                                 reads=[pb[bd]], writes=[b_ad], selfsync=False)

                nchunk0 = nchunk
                emit_scores(0)
                for ci in range(len(chunks)):
                    if ci + 1 < len(chunks):
                        emit_scores(ci + 1)
                    emit_pv(ci)
                nchunk += len(chunks)
                S.op("dve", lambda e: e.reciprocal(accd[:], accd[:]), reads=[b_ad], writes=[b_ad])
                S.op("dve", lambda e, h=h: e.tensor_tensor(aallv[:, h, :], accn[:], accd[:], ALU.mult),
                     reads=[b_an, b_ad], writes=[aallb[h]])
            rms_merge(aallv, aallb, lambda g: cpar[:, 32 + g:32 + g + 1], 0,
                      [accn[:, :], accd[:, :]], [b_an, b_ad],
                      vt[0][:, 0:2 * TOK_OWN].bitcast(F32), vtb[0],
                      [vt[1][:, 0:TOK_OWN], vt[1][:, TOK_OWN:2 * TOK_OWN]], [vtb[1], vtb[1]])
            S.barrier()

        if stop_after == "p2":
            return nc

        with ExitStack() as p4:
            W = {}
            hT, hTb, xst, xstb = alloc_ffn_work(W, p4, "b")
            gvecB = sb("gvecB", [128, D], F32, p4)
            b_gB = S.buf("gB")
            mtv = W["actT"][:, 0:KC, :]
            mtb = W["actb"][0:KC]
            x2b = S.bufs(NSUB, "x2d")
            S.dma("sp", gvecB[:], g2row_d[0:1, :].partition_broadcast(128), writes=[b_gB])
            S.dma("sp", gvecA[:], g3row_d[0:1, :].partition_broadcast(128), writes=[b_gA])
            merged_v = merged_d.rearrange("(c p) t -> p c t", p=128)
            for i in range(dbg.get("n_tiles4", NT_OWN)):
                r0 = i * T
                S.dma("sp", mtv, merged_v[:, :, r0:r0 + T], writes=mtb)

                def epi_w(s, r0=r0):
                    xs_, xb_ = xst[s % 2], xstb[s % 2]
                    S.dma("sp", xs_[:], x1_d[r0 + s * 128:r0 + (s + 1) * 128, :], writes=[xb_])
                    yield
                    yield from postnorm_residual_g(s, W, gvecB, b_gB, xs_, xb_)
                    S.dma("sp", x2_d[r0 + s * 128:r0 + (s + 1) * 128, :], xs_[:], reads=[xb_], writes=[x2b[s]])
                    yield from prenorm_A_g(xs_, xb_, s, W)

                proj_down(W, mtv, mtb, KC, "wout", ((0, KC),),
                          (epi_w, lambda s: prenorm_B(2, s, hT, hTb, W), lambda s: None))
                ffn_gateup(hT, hTb, W)

                def epi2(s, r0=r0):
                    xs_, xb_ = xst[s % 2], xstb[s % 2]
                    S.dma("sp", xs_[:], x2_d[r0 + s * 128:r0 + (s + 1) * 128, :], reads=[x2b[s]], writes=[xb_])
                    yield
                    yield from postnorm_residual_g(s, W, gvecA, b_gA, xs_, xb_)
                    S.dma("sp", out_d[r0 + s * 128:r0 + (s + 1) * 128, :], xs_[:], reads=[xb_])
                    yield

                ffn_down(W, (epi2, lambda s: None, lambda s: None))
            S.barrier()
    return nc


def _cols(v, n):
    return np.ascontiguousarray(np.asarray(v, np.float32).reshape(n, 128).T)


def prep_core_inputs(core, inp, shared):
    b = core // 2
    rev = core % 2 == 1
    x = inp["x"][b]
    S = x.shape[0]
    if not rev:
        gpos = np.arange(0, TOK_ALL)
    else:
        gpos = S - 1 - np.arange(0, TOK_ALL)
    m = dict(shared)
    m["x"] = np.ascontiguousarray(x[gpos])
    m["c_col"] = _cols(inp["c"][b], KC)
    cw = np.asarray(inp["conv_w"][0], np.float32)
    if rev:
        cw = cw[::-1]
    m["conv_w_cols"] = np.ascontiguousarray(
        cw.T.reshape(8, 128, 31).transpose(1, 0, 2).reshape(128, 8 * 31))
    half = 64
    inv_freq = (10000.0 ** (-np.arange(half, dtype=np.float32) / half)).astype(np.float32)
    ang = gpos.astype(np.float32)[None, :] * inv_freq[:, None]
    cos = np.cos(ang).astype(np.float32)
    sin = np.sin(ang).astype(np.float32)
    m["rope_cos"] = np.ascontiguousarray(np.concatenate([cos, cos], 0))
    m["rope_sin"] = np.ascontiguousarray(np.concatenate([-sin, sin], 0))
    return m


def prep_shared(inp):
    sh = {}
    for k in ("w_ada", "ffn1_w_gate", "ffn1_w_up", "ffn1_w_down", "ffn2_w_gate", "ffn2_w_up",
              "ffn2_w_down", "w_in", "w_out"):
        sh[k] = np.ascontiguousarray(np.asarray(inp[k], np.float32)[0])
    sh["b_ada"] = np.ascontiguousarray(np.asarray(inp["b_ada"], np.float32).reshape(1, -1))
    sh["pre_g_cols"] = np.ascontiguousarray(np.concatenate(
        [_cols(inp[k][0], KC) for k in ("ffn1_pre_g", "mix_pre_g", "ffn2_pre_g")], 1))
    sh["post_g_rows"] = np.ascontiguousarray(np.concatenate(
        [np.asarray(inp[k][0], np.float32) for k in ("ffn1_post_g", "mix_post_g", "ffn2_post_g")])[None, :])
    sh["cpar_cols"] = np.ascontiguousarray(np.concatenate(
        [_cols(inp[k][0], 8) for k in ("conv_b", "conv_ln_g", "conv_ln_b", "conv_out_g", "attn_out_g")], 1))
    sh["ident"] = np.eye(128, dtype=np.float32).astype(ml_dtypes.bfloat16)
    a = np.arange(128)[:, None]
    bq = np.arange(128)[None, :]
    NEG = -30000.0
    mB = np.where(a <= bq, 0.0, NEG)
    mA = np.where(a >= bq, 0.0, NEG)
    normal = np.concatenate([mB, mA], 1)
    edge = normal.copy()
    edge[:64, :] = NEG
    sh["maskb"] = np.ascontiguousarray(np.concatenate([normal, edge], 1).astype(np.float32)).astype(ml_dtypes.bfloat16)
    return sh


def kernel(**inputs):
    inp = {k: np.asarray(v) for k, v in inputs.items()}
    B, S, Dm = inp["x"].shape
    shared = prep_shared(inp)
    in_maps = [prep_core_inputs(c, inp, shared) for c in range(8)]
    nc = build_nc()
    res = run_bass_kernel_spmd(nc, in_maps, core_ids=list(range(8)))
    out = np.empty((B, S, Dm), np.float32)
    for c in range(8):
        o = res.results[c]["out"]
        b = c // 2
        if c % 2 == 0:
            out[b, 0:TOK_OWN] = o
        else:
            out[b, S - 1 - np.arange(TOK_OWN)] = o
    return out
```

```python
import numpy as np
import ml_dtypes
from contextlib import ExitStack
import concourse.bass as bass
import concourse.mybir as mybir
from concourse.bass_utils import run_bass_kernel_spmd

F32 = mybir.dt.float32
BF16 = mybir.dt.bfloat16
ALU = mybir.AluOpType
AF = mybir.ActivationFunctionType
AX = mybir.AxisListType

D = 2048
KC = 16
FF = 5632
FC = 44
T = 512
NSUB = 4
TOK_ALL = 3072
TOK_OWN = 2048
NT_ALL = TOK_ALL // T
NT_OWN = TOK_OWN // T
EPS = 1e-6
UPAD = 16
UW = UPAD + 2560
HEADS = 8
SCALE = 128 ** -0.5


class Buf:
    __slots__ = ("name", "w", "r", "dsem", "dcnt", "const")

    def __init__(self, name):
        self.name = name
        self.w = None
        self.r = {}
        self.dsem = None
        self.dcnt = 0
        self.const = False


class Sched:
    ENG = ("pe", "act", "dve", "pool", "sp")

    def __init__(self, nc, stack):
        self.nc = nc
        self.stack = stack
        self.h = {"pe": nc.tensor, "act": nc.scalar, "dve": nc.vector,
                  "pool": nc.gpsimd, "sp": nc.sync}
        self.sem = {}
        self.cnt = {}
        self.semobj = {}
        for e in self.ENG:
            s = stack.enter_context(nc.semaphore("s_" + e))
            self.sem[e] = s
            self.cnt[e] = 0
            self.semobj[e] = s
        self.seen = {e: {} for e in self.ENG}
        self.nbuf = 0
        self.ndsem = 0
        self.dbufs = []
        self._bykey = {}

    def buf(self, name=None):
        self.nbuf += 1
        return Buf(name or ("b%d" % self.nbuf))

    def bufs(self, n, name="b"):
        return [self.buf("%s%d" % (name, i)) for i in range(n)]

    def _dsem(self, b):
        key = ("d", id(b))
        if b.dsem is None:
            self.ndsem += 1
            b.dsem = self.stack.enter_context(self.nc.semaphore("d%d" % self.ndsem))
            self.semobj[key] = b.dsem
            self.dbufs.append(b)
            self._bykey[key] = b
        return key

    def _wait(self, e, key, count):
        if self.seen[e].get(key, 0) >= count:
            return
        self.seen[e][key] = count
        self.h[e].wait_ge(self.semobj[key], count)

    def _deps(self, e, reads, writes, selfsync, skip_key=None):
        deps = {}

        def add(k, c):
            if isinstance(k, tuple):
                c = max(c, self._bykey[k].dcnt)
            if deps.get(k, 0) < c:
                deps[k] = c
        for b in reads:
            if b.w is not None:
                add(*b.w)
        for b in writes:
            if b.w is not None and b.w[0] != skip_key:
                add(*b.w)
            for k, c in b.r.items():
                add(k, c)
        for k, c in deps.items():
            if k == e:
                if not selfsync or e == "pe":
                    continue
            self._wait(e, k, c)

    def _record(self, rec, reads, writes):
        k, c = rec
        for b in reads:
            if not b.const:
                if b.r.get(k, 0) < c:
                    b.r[k] = c
        for b in writes:
            b.w = rec
            b.r = {}

    def op(self, e, fn, reads=(), writes=(), selfsync=True):
        self._deps(e, reads, writes, selfsync)
        ins = fn(self.h[e])
        self.cnt[e] += 1
        ins.then_inc(self.sem[e], 1)
        self._record((e, self.cnt[e]), reads, writes)
        return ins

    def dma(self, q, out, in_, reads=(), writes=(), sembuf=None, **kw):
        sb = sembuf or (writes[0] if writes else reads[0])
        key = self._dsem(sb)
        self._deps(q, reads, writes, False, skip_key=key)
        ins = self.h[q].dma_start(out=out, in_=in_, **kw)
        sb.dcnt += 16
        ins.then_inc(sb.dsem, 16)
        self._record((key, sb.dcnt), reads, writes)
        return ins

    def wait_all(self, e, bufs):
        self._deps(e, [], bufs, False)

    def barrier(self):
        for e in self.ENG:
            for k in self.ENG:
                if k != e and self.cnt[k] > 0:
                    self._wait(e, k, self.cnt[k])
            for b in self.dbufs:
                if b.dcnt > 0:
                    self._wait(e, ("d", id(b)), b.dcnt)


class Unit:
    __slots__ = ("tag", "parts", "cache", "ncols")

    def __init__(self, tag, parts, cache=None):
        self.tag = tag
        self.parts = parts
        self.cache = cache
        self.ncols = max(off + a * b for (off, a, b, _) in parts)


class WRing:
    def __init__(self, S, nc, stack, nslots, plan):
        self.S = S
        self.n = nslots
        self.slots = [stack.enter_context(nc.sbuf_tensor("wr%d" % i, [128, 8192], BF16))
                      for i in range(nslots)]
        self.bufs = S.bufs(nslots, "wr")
        self.plan = plan
        self.issued = 0
        self.cur = 0
        self.cache_d = None
        self.cache_rec = {}

    def next(self, tag, hold=0):
        k = self.cur
        assert self.plan[k].tag == tag, (k, self.plan[k].tag, tag)
        lim = min(len(self.plan), k - hold + self.n)
        S = self.S
        while self.issued < lim:
            j = self.issued
            slot = self.slots[j % self.n]
            b = self.bufs[j % self.n]
            u = self.plan[j]
            if j >= self.n:
                old = self.plan[j - self.n]
                if old.cache is not None and old.cache[0] == "store":
                    idx = old.cache[1]
                    S.dma("pool", self.cache_d[idx * 128:(idx + 1) * 128, 0:old.ncols], slot[:, 0:old.ncols],
                          reads=[b], sembuf=b)
                    self.cache_rec[idx] = (("d", id(b)), b.dcnt)
            if u.cache is not None and u.cache[0] == "load":
                idx = u.cache[1]
                key, cnt = self.cache_rec[idx]
                S._wait("pool", key, cnt)
                S.dma("pool", slot[:, 0:u.ncols], self.cache_d[idx * 128:(idx + 1) * 128, 0:u.ncols], writes=[b])
            else:
                for (off, a, bb, src) in u.parts:
                    dst = slot[:, off:off + a * bb].rearrange("p (a b) -> p a b", a=a)
                    S.dma("pool", dst, src, writes=[b])
            self.issued += 1
        self.cur += 1
        return self.slots[k % self.n], self.bufs[k % self.n]


FGROUPS = ((0, 16), (16, 32), (32, 44))


def ffn_units(wg_v, wu_v, wd_v):
    us = []
    for u in range(FC // 2):
        us.append(Unit("gu", [(0, KC, 256, wg_v[:, :, u * 256:(u + 1) * 256]),
                              (4096, KC, 256, wu_v[:, :, u * 256:(u + 1) * 256])]))
    for n in range(4):
        for (f0, f1) in FGROUPS:
            us.append(Unit("dn", [(0, f1 - f0, 512, wd_v[:, f0:f1, n * 512:(n + 1) * 512])]))
    return us


def build_nc(dbg=None):
    dbg = dbg or {}
    n_tiles = dbg.get("n_tiles", NT_ALL)
    stop_after = dbg.get("stop_after", "all")

    nc = bass.Bass("TRN2", target_bir_lowering=False)

    def din(name, shape, dt=F32):
        return nc.dram_tensor(name, shape, dt, kind="ExternalInput").ap()

    def dscr(name, shape, dt=F32):
        kind = "ExternalOutput" if dbg.get("expose") else "Internal"
        return nc.dram_tensor(name, shape, dt, kind=kind).ap()

    x_d = din("x", [TOK_ALL, D])
    ccol_d = din("c_col", [128, KC])
    wada_d = din("w_ada", [D, 9 * D])
    bada_d = din("b_ada", [1, 9 * D])
    pregc_d = din("pre_g_cols", [128, 3 * KC])
    postg_d = din("post_g_rows", [1, 3 * D])
    f1g_d = din("ffn1_w_gate", [D, FF])
    f1u_d = din("ffn1_w_up", [D, FF])
    f1d_d = din("ffn1_w_down", [FF, D])
    f2g_d = din("ffn2_w_gate", [D, FF])
    f2u_d = din("ffn2_w_up", [D, FF])
    f2d_d = din("ffn2_w_down", [FF, D])
    win_d = din("w_in", [D, 5120])
    wout_d = din("w_out", [D, D])
    convw_d = din("conv_w_cols", [128, 8 * 31])
    cpar_d = din("cpar_cols", [128, 5 * 8])
    cos_d = din("rope_cos", [128, TOK_ALL])
    sin_d = din("rope_sin", [128, TOK_ALL])
    ident_d = din("ident", [128, 128], BF16)
    maskb_d = din("maskb", [128, 2 * 256], BF16)
    out_d = nc.dram_tensor("out", [TOK_OWN, D], F32, kind="ExternalOutput").ap()

    x1_d = dscr("x1_s", [TOK_OWN, D])
    qT_d = dscr("qT_s", [HEADS * 128, TOK_OWN], BF16)
    KPAD = 1024
    kT_d = dscr("kT_s", [HEADS * 128, KPAD + TOK_ALL], BF16)
    v_d = dscr("v_s", [KPAD + TOK_ALL, 1024], BF16)
    merged_d = dscr("merged_s", [16 * 128, TOK_OWN], BF16)
    x2_d = dscr("x2_s", [TOK_OWN, D])
    u_d = dscr("u_s", [8 * 128, UW], BF16)
    g2row_d = dscr("g2row_s", [1, D])
    g3row_d = dscr("g3row_s", [1, D])

    wada_v = wada_d.rearrange("(kc p) f -> p kc f", p=128)
    win_v = win_d.rearrange("(kc p) f -> p kc f", p=128)
    wout_v = wout_d.rearrange("(kc p) f -> p kc f", p=128)
    f1 = (f1g_d.rearrange("(kc p) f -> p kc f", p=128), f1u_d.rearrange("(kc p) f -> p kc f", p=128),
          f1d_d.rearrange("(fc p) d -> p fc d", p=128))
    f2 = (f2g_d.rearrange("(kc p) f -> p kc f", p=128), f2u_d.rearrange("(kc p) f -> p kc f", p=128),
          f2d_d.rearrange("(fc p) d -> p fc d", p=128))

    def win_unit(j):
        return Unit("win", [(0, KC, 512, win_v[:, :, j * 512:(j + 1) * 512])])

    def conv_unit(j):
        return Unit("cv", [(0, KC, 256, win_v[:, :, 3072 + 256 * j:3072 + 256 * (j + 1)]),
                           (4096, KC, 256, win_v[:, :, 4096 + 256 * j:4096 + 256 * (j + 1)])])

    plan = []
    NV0 = 5
    NVA = 2
    for j in range(4 * NVA):
        plan.append(Unit("ada", [(0, KC, 512, wada_v[:, :, j * 512:(j + 1) * 512])]))
    N_INTER = 4 * (NV0 - NVA)
    ncache = [0]
    cmap = {}

    def cached(units, group, first):
        out = []
        for q, u in enumerate(units):
            key = (group, q)
            if first:
                cmap[key] = ncache[0]
                ncache[0] += 1
                u.cache = ("store", cmap[key])
            else:
                u.cache = ("load", cmap[key])
            out.append(u)
        return out

    use_cache = dbg.get("cache", False)
    if stop_after != "p0":
        for i in range(n_tiles):
            fu = ffn_units(*f1)
            wu = [win_unit(j) for j in range(6)]
            cu = [conv_unit(j) for j in range(4)]
            if use_cache:
                fu = cached(fu, "f1", i == 0)
                wu = cached(wu, "win", i == 0)
                cu = cached(cu, "cv", i == 0)
            if i == 0:
                fu2 = []
                for q, u_ in enumerate(fu):
                    fu2.append(u_)
                    if q < N_INTER:
                        jj = 4 * NVA + q
                        fu2.append(Unit("ada", [(0, KC, 512, wada_v[:, :, jj * 512:(jj + 1) * 512])]))
                fu = fu2
            plan += fu
            plan += (wu[4:6] + wu[0:4]) if i < NT_OWN else (wu[4:6] + wu[2:4])
            if i <= NT_OWN:
                plan += cu
        for j in range(4 * NV0, 36):
            plan.append(Unit("ada", [(0, KC, 512, wada_v[:, :, j * 512:(j + 1) * 512])]))
    if stop_after == "all":
        for i in range(NT_OWN):
            ou = [Unit("wout", [(0, KC, 512, wout_v[:, :, n * 512:(n + 1) * 512])]) for n in range(4)]
            fu = ffn_units(*f2)
            if use_cache:
                ou = cached(ou, "wo", i == 0)
                fu = cached(fu, "f2", i == 0)
            plan += ou
            plan += fu
    cache_d = dscr("wcache_s", [max(ncache[0], 1) * 128, 8192], BF16)

    with ExitStack() as st:
        S = Sched(nc, st)

        def sb(name, shape, dt=F32, stack=st):
            return stack.enter_context(nc.sbuf_tensor("s_" + name, shape, dt))

        pa = [st.enter_context(nc.psum_tensor("pa%d" % i, [128, 512], F32)) for i in range(8)]
        pb = S.bufs(8, "pb")
        rot = {"acc": 0, "tr": 0}

        def rot6():
            i = rot["acc"]
            rot["acc"] = (i + 1) % 6
            return i

        def rot2():
            i = 6 + rot["tr"]
            rot["tr"] = (rot["tr"] + 1) % 2
            return i

        ident = sb("ident", [128, 128], BF16)
        ones_f = sb("ones_f", [128, 128])
        modcols = sb("modcols", [128, 6 * KC])
        pregc = sb("pregc", [128, 3 * KC])
        gvecA = sb("gvecA", [128, D])
        b_ident, b_ones, b_mod, b_pregc, b_gA = S.bufs(5, "c")

        ring = WRing(S, nc, st, dbg.get("nslots", 4), plan)
        ring.cache_d = cache_d

        S.dma("sp", ident[:], ident_d[:, :], writes=[b_ident])
        S.dma("sp", pregc[:], pregc_d[:, :], writes=[b_pregc])
        S.op("dve", lambda e: e.memset(ones_f[:], 1.0), writes=[b_ones])

        ccol = sb("ccol", [128, KC], F32)
        cact = sb("cact", [128, KC], BF16)
        b_ccol, b_cact = S.bufs(2, "p0")
        S.dma("sp", ccol[:], ccol_d[:, :], writes=[b_ccol])
        S.op("act", lambda e: e.activation(cact[:], ccol[:], AF.Silu), reads=[b_ccol], writes=[b_cact])

        def mod_vectors(vs, stk, views=None, vbufs=None):
            tg = "m%d" % vs[0]
            if views is None:
                mrow = [sb("mrow%s%d" % (tg, i), [1, 512], F32, stk) for i in range(2)]
                brow = [sb("brow%s%d" % (tg, i), [1, 512], F32, stk) for i in range(2)]
                prow = [sb("prow%s%d" % (tg, i), [1, 512], F32, stk) for i in range(2)]
                grow = [sb("grow%s%d" % (tg, i), [1, 512], F32, stk) for i in range(2)]
                tmpc = sb("tmpc" + tg, [128, KC], F32, stk)
            else:
                rows, tmpc = views
                mrow, brow, prow, grow = rows[0:2], rows[2:4], rows[4:6], rows[6:8]
            b_mrow = S.bufs(2, "mrow")
            b_brow = S.bufs(2, "brow")
            b_prow = S.bufs(2, "prow")
            b_grow = S.bufs(2, "grow")
            b_tmpc = S.buf("tmpc")
            if vbufs is not None:
                vbufs.extend(b_mrow + b_brow + b_prow + b_grow + [b_tmpc])
            np_ = 0
            for v in vs:
                k = v // 3
                kind = v % 3
                bkc = (6 + v % 2) if kind in (0, 1) else None
                for j in range(4):
                    pi = np_ % 2
                    np_ += 1
                    c0 = v * D + j * 512
                    S.dma("sp", brow[pi][:], bada_d[0:1, c0:c0 + 512], writes=[b_brow[pi]])
                    slot, sbf = ring.next("ada")
                    wv = slot[:, :].rearrange("p (a b) -> p a b", a=KC)
                    bk = rot6()
                    for kc in range(KC):
                        S.op("pe", lambda e, kc=kc, bk=bk: e.matmul(
                            pa[bk][0:1, :], cact[:, kc:kc + 1], wv[:, kc, :],
                            start=(kc == 0), stop=False),
                            reads=[b_cact, sbf], writes=[pb[bk]])
                    S.op("pe", lambda e, bk=bk, pi=pi: e.matmul(
                        pa[bk][0:1, :], ones_f[0:1, 0:1], brow[pi][0:1, :], start=False, stop=True),
                        reads=[b_ones, b_brow[pi]], writes=[pb[bk]])
                    S.op("act", lambda e, bk=bk, pi=pi: e.copy(mrow[pi][0:1, :], pa[bk][0:1, :]),
                         reads=[pb[bk]], writes=[b_mrow[pi]])
                    if kind in (0, 1):
                        for q in range(4):
                            kc = j * 4 + q
                            S.op("pe", lambda e, kc=kc, q=q, pi=pi: e.matmul(
                                pa[bkc][:, kc:kc + 1], mrow[pi][0:1, q * 128:(q + 1) * 128], ones_f[0:1, 0:1],
                                start=True, stop=True), reads=[b_mrow[pi], b_ones], writes=[pb[bkc]])
                    else:
                        half = 0.5 if k in (0, 2) else 1.0
                        S.dma("sp", prow[pi][:], postg_d[0:1, k * D + j * 512:k * D + (j + 1) * 512],
                              writes=[b_prow[pi]])
                        S.op("pool", lambda e, pi=pi: e.tensor_tensor(
                            grow[pi][0:1, :], mrow[pi][0:1, :], prow[pi][0:1, :], ALU.mult),
                            reads=[b_mrow[pi], b_prow[pi]], writes=[b_grow[pi]])
                        S.op("pool", lambda e, pi=pi, half=half: e.tensor_scalar(
                            grow[pi][0:1, :], grow[pi][0:1, :], half, None, ALU.mult),
                            reads=[b_grow[pi]], writes=[b_grow[pi]])
                        if k == 0:
                            bk2 = rot6()
                            S.op("pe", lambda e, bk2=bk2, pi=pi: e.matmul(
                                pa[bk2][:, :], ones_f[0:1, :], grow[pi][0:1, :],
                                start=True, stop=True), reads=[b_grow[pi], b_ones], writes=[pb[bk2]])
                            S.op("act", lambda e, j=j, bk2=bk2: e.copy(
                                gvecA[:, j * 512:(j + 1) * 512], pa[bk2][:, :]),
                                reads=[pb[bk2]], writes=[b_gA])
                        else:
                            S.dma("sp", (g2row_d if k == 1 else g3row_d)[0:1, j * 512:(j + 1) * 512],
                                  grow[pi][0:1, :], reads=[b_grow[pi]])
                    if j < 3:
                        yield
                if kind == 0:
                    S.op("act", lambda e, k=k: e.copy(
                        modcols[:, (2 * k + 1) * KC:(2 * k + 2) * KC], pa[bkc][:, 0:KC]),
                        reads=[pb[bkc]], writes=[b_mod])
                elif kind == 1:
                    S.op("act", lambda e, k=k: e.activation(
                        tmpc[:, :], pa[bkc][:, 0:KC], AF.Identity, bias=1.0, scale=1.0),
                        reads=[pb[bkc]], writes=[b_tmpc])
                    S.op("pool", lambda e, k=k: e.tensor_tensor(
                        modcols[:, (2 * k) * KC:(2 * k + 1) * KC], tmpc[:, :],
                        pregc[:, k * KC:(k + 1) * KC], ALU.mult),
                        reads=[b_tmpc, b_pregc], writes=[b_mod])
                yield

        with ExitStack() as p0:
            for _ in mod_vectors(list(range(NVA)), p0):
                pass
            S.barrier()
        b_ident.const = True
        b_ones.const = True

        if stop_after == "p0":
            dbg_mod = nc.dram_tensor("dbg_mod", [128, 6 * KC], F32, kind="ExternalOutput").ap()
            dbg_g = nc.dram_tensor("dbg_g", [128, D], F32, kind="ExternalOutput").ap()
            S.dma("sp", dbg_mod[:, :], modcols[:], reads=[b_mod])
            S.dma("sp", dbg_g[:, :], gvecA[:], reads=[b_gA])
            S.barrier()
            return nc


        def mcol(k, which, kc):
            c0 = (2 * k + which) * KC + kc
            return modcols[:, c0:c0 + 1]

        def prenorm_A_g(xs_, xb_, s, W):
            S.op("act", lambda e: e.activation(W["xn"][s % 2][:], xs_[:], AF.Square,
                                               accum_out=W["pss"][:, s:s + 1]),
                 reads=[xb_], writes=[W["pssb"][s], W["xnb"][s % 2]])
            yield
            S.op("act", lambda e: e.activation(W["pss"][:, s:s + 1], W["pss"][:, s:s + 1], AF.Sqrt,
                                               bias=EPS, scale=1.0 / D),
                 reads=[W["pssb"][s]], writes=[W["pssb"][s]])
            yield
            S.op("dve", lambda e: e.reciprocal(W["pss"][:, s:s + 1], W["pss"][:, s:s + 1]),
                 reads=[W["pssb"][s]], writes=[W["pssb"][s]])
            yield
            xn = W["xn"][s % 2]
            xnb = W["xnb"][s % 2]
            S.op("dve", lambda e: e.tensor_scalar(xn[:], xs_[:], W["pss"][:, s:s + 1], None, ALU.mult),
                 reads=[xb_, W["pssb"][s]], writes=[xnb])
            yield

        def prenorm_A(xs_, xb_, s, W):
            for _ in prenorm_A_g(xs_, xb_, s, W):
                pass

        def prenorm(xs_, xb_, k, s, hT, hTb, W):
            prenorm_A(xs_, xb_, s, W)
            prenorm_B(k, s, hT, hTb, W)

        def prenorm_B(k, s, hT, hTb, W):
            xn = W["xn"][s % 2]
            xnb = W["xnb"][s % 2]
            for g4 in range(4):
                bk = rot2()
                ptb = pa[bk][:].bitcast(BF16)
                for j in range(4):
                    kc = g4 * 4 + j
                    S.op("pe", lambda e, j=j, kc=kc: e.transpose(
                        ptb[:, j * 128:(j + 1) * 128], xn[:, kc * 128:(kc + 1) * 128], ident[:]),
                        reads=[xnb, b_ident], writes=[pb[bk]])
                for j in range(4):
                    kc = g4 * 4 + j
                    dst = hT[:, kc, s * 128:(s + 1) * 128]
                    if g4 % 2 == 0:
                        S.op("dve", lambda e, j=j, kc=kc, dst=dst: e.tensor_scalar(
                            dst, ptb[:, j * 128:(j + 1) * 128], mcol(k, 0, kc), mcol(k, 1, kc),
                            ALU.mult, ALU.add), reads=[pb[bk], b_mod], writes=[hTb[s]])
                    else:
                        S.op("act", lambda e, j=j, kc=kc, dst=dst: e.activation(
                            dst, ptb[:, j * 128:(j + 1) * 128], AF.Identity,
                            bias=mcol(k, 1, kc), scale=mcol(k, 0, kc)),
                            reads=[pb[bk], b_mod], writes=[hTb[s]])

        def ffn_gateup(hT, hTb, W, hook=None):
            for u in range(FC // 2):
                if hook is not None and u >= 1:
                    hook(u - 1)
                slot, sbf = ring.next("gu")
                wg = slot[:, 0:4096].rearrange("p (a b) -> p a b", a=KC)
                wu = slot[:, 4096:8192].rearrange("p (a b) -> p a b", a=KC)
                for mm in range(2):
                    m = 2 * u + mm
                    ig = rot6()
                    iu = rot6()
                    for kc in range(KC):
                        S.op("pe", lambda e, kc=kc: e.matmul(
                            pa[ig][:, :], wg[:, kc, mm * 128:(mm + 1) * 128], hT[:, kc, :],
                            start=(kc == 0), stop=(kc == KC - 1)),
                            reads=[sbf] + hTb, writes=[pb[ig]])
                    for kc in range(KC):
                        S.op("pe", lambda e, kc=kc: e.matmul(
                            pa[iu][:, :], wu[:, kc, mm * 128:(mm + 1) * 128], hT[:, kc, :],
                            start=(kc == 0), stop=(kc == KC - 1)),
                            reads=[sbf] + hTb, writes=[pb[iu]])
                    sg = W["sgt"][m % 2]
                    sgb = W["sgb"][m % 2]
                    S.op("act", lambda e: e.activation(sg[:], pa[ig][:, :], AF.Silu),
                         reads=[pb[ig]], writes=[sgb])
                    S.op("dve", lambda e: e.tensor_tensor(W["actT"][:, m, :], sg[:], pa[iu][:, :], ALU.mult),
                         reads=[sgb, pb[iu]], writes=[W["actb"][m]])

        def alloc_ffn_work(W, stk, tg):
            hT_t = sb("hT" + tg, [128, KC * T], BF16, stk)
            hT = hT_t[:, :].rearrange("p (a b) -> p a b", a=KC)
            hTb = S.bufs(NSUB, "hT")
            actT_t = sb("actT" + tg, [128, FC * T], BF16, stk)
            W["actT"] = actT_t[:, :].rearrange("p (a b) -> p a b", a=FC)
            W["actb"] = S.bufs(FC, "act")
            W["f_sb"] = [sb("f_sb%s%d" % (tg, i), [128, D], F32, stk) for i in range(NSUB)]
            W["fb"] = S.bufs(NSUB, "f")
            xst = [sb("xst%s%d" % (tg, i), [128, D], F32, stk) for i in range(2)]
            xstb = S.bufs(2, "xst")
            W["xn"] = [sb("xn%s%d" % (tg, i), [128, D], BF16, stk) for i in range(2)]
            W["xnb"] = S.bufs(2, "xn")
            W["sgt"] = [sb("sgt%s%d" % (tg, i), [128, T], F32, stk) for i in range(2)]
            W["sgb"] = S.bufs(2, "sg")
            W["pss"] = sb("pss" + tg, [128, 8], F32, stk)
            W["pssb"] = S.bufs(8, "pss")
            W["prs"] = sb("prs" + tg, [128, 8], F32, stk)
            W["prsb"] = S.bufs(8, "prs")
            return hT, hTb, xst, xstb

        def proj_down(W, lhs, lhsb, nch, units_tag, groups, after_sub=None):
            def evac(s, n, bk):
                if s % 2 == 0:
                    S.op("dve", lambda e: e.tensor_copy(
                        W["f_sb"][s][:, n * 512:(n + 1) * 512], pa[bk][:, :]),
                        reads=[pb[bk]], writes=[W["fb"][s]])
                else:
                    S.op("act", lambda e: e.copy(
                        W["f_sb"][s][:, n * 512:(n + 1) * 512], pa[bk][:, :]),
                        reads=[pb[bk]], writes=[W["fb"][s]])

            for n in range(4):
                banks = [rot6() for _ in range(NSUB)]
                if n < 3:
                    for (f0, f1) in groups:
                        slot, sbf = ring.next(units_tag)
                        wd = slot[:, 0:(f1 - f0) * 512].rearrange("p (a b) -> p a b", b=512)
                        for s in range(NSUB):
                            for fc in range(f0, f1):
                                S.op("pe", lambda e, s=s, fc=fc: e.matmul(
                                    pa[banks[s]][:, :], lhs[:, fc, s * 128:(s + 1) * 128], wd[:, fc - f0, :],
                                    start=(fc == 0), stop=(fc == nch - 1)),
                                    reads=[lhsb[fc], sbf], writes=[pb[banks[s]]])
                    for s in range(NSUB):
                        evac(s, n, banks[s])
                else:
                    wds = []
                    for gi, (f0, f1) in enumerate(groups):
                        slot, sbf = ring.next(units_tag, hold=gi)
                        wds.append((slot[:, 0:(f1 - f0) * 512].rearrange("p (a b) -> p a b", b=512), sbf, f0, f1))
                    for s in range(NSUB):
                        for (wd, sbf, f0, f1) in wds:
                            for fc in range(f0, f1):
                                S.op("pe", lambda e, s=s, fc=fc, wd=wd, f0=f0: e.matmul(
                                    pa[banks[s]][:, :], lhs[:, fc, s * 128:(s + 1) * 128], wd[:, fc - f0, :],
                                    start=(fc == 0), stop=(fc == nch - 1)),
                                    reads=[lhsb[fc], sbf], writes=[pb[banks[s]]])
                        evac(s, n, banks[s])
                        if after_sub is not None:
                            epiA, epiB, epiX = after_sub
                            if s == 1:
                                run_interleaved([epiA(0), epiA(1)])
                            elif s == 3:
                                epiB(0)
                                epiB(1)
                                run_interleaved([epiA(2), epiA(3)])
                                epiX(0)
                                epiX(1)
                                epiB(2)
                                epiB(3)
                                epiX(2)
                                epiX(3)

        def ffn_down(W, after_sub=None):
            proj_down(W, W["actT"], W["actb"], FC, "dn", FGROUPS, after_sub)

        def postnorm_residual_g(s, W, gvec, gvb, xs_, xb_):
            r = W["prs"][:, s:s + 1]
            rb = W["prsb"][s]
            S.op("act", lambda e: e.activation(W["xn"][s % 2][:], W["f_sb"][s][:], AF.Square, accum_out=r),
                 reads=[W["fb"][s]], writes=[rb, W["xnb"][s % 2]])
            yield
            S.op("act", lambda e: e.activation(r, r, AF.Sqrt, bias=EPS, scale=1.0 / D), reads=[rb], writes=[rb])
            yield
            S.op("dve", lambda e: e.reciprocal(r, r), reads=[rb], writes=[rb])
            yield
            f = W["f_sb"][s]
            S.op("dve", lambda e: e.scalar_tensor_tensor(f[:], f[:], r, gvec[:], ALU.mult, ALU.mult),
                 reads=[W["fb"][s], rb, gvb], writes=[W["fb"][s]])
            yield
            S.op("dve", lambda e: e.tensor_tensor(xs_[:], xs_[:], f[:], ALU.add),
                 reads=[xb_, W["fb"][s]], writes=[xb_])
            yield

        def postnorm_residual(s, W, gvec, gvb, xs_, xb_):
            for _ in postnorm_residual_g(s, W, gvec, gvb, xs_, xb_):
                pass

        def run_interleaved(gens):
            gens = list(gens)
            while gens:
                for g in list(gens):
                    try:
                        next(g)
                    except StopIteration:
                        gens.remove(g)

        with ExitStack() as p1:
            W = {}
            print("sbuf remaining at P1 start", nc.sbuf_bytes_remaining)
            hT, hTb, xst, xstb = alloc_ffn_work(W, p1, "a")
            cosT = sb("cosT", [128, T], F32, p1)
            sinT = sb("sinT", [128, T], F32, p1)
            b_cos, b_sin = S.bufs(2, "cs")
            tmpA, tmpB = W["sgt"]
            b_tA, b_tB = W["sgb"]
            print("sbuf remaining before staging", nc.sbuf_bytes_remaining)
            qst = [sb("qst%d" % i, [128, T], BF16, p1) for i in range(2)]
            qstb = S.bufs(2, "qst")
            vst = [sb("vst%d" % i, [128, T], BF16, p1) for i in range(2)]
            vstb = S.bufs(2, "vst")
            ust = [sb("ust%d" % i, [128, T], BF16, p1) for i in range(2)]
            ustb = S.bufs(2, "ust")
            zt = sb("zt", [128, UPAD], BF16, p1)
            b_zt = S.buf("zt")
            S.op("dve", lambda e: e.memset(zt[:], 0.0), writes=[b_zt])
            for cg in range(8):
                S.dma("sp", u_d[cg * 128:(cg + 1) * 128, 0:UPAD], zt[:], reads=[b_zt])
            zb = W["xn"][0]
            S.op("dve", lambda e: e.memset(zb[:, 0:KPAD], 0.0), writes=[W["xnb"][0]])
            for hh in range(8):
                S.dma("sp", kT_d[hh * 128:(hh + 1) * 128, 0:KPAD], zb[:, 0:KPAD], reads=[W["xnb"][0]])
                S.dma("sp", v_d[hh * 128:(hh + 1) * 128, :], zb[:, 0:1024], reads=[W["xnb"][0]])

            cnt = {"q": 0, "v": 0, "u": 0}
            for i in range(n_tiles):
                r0 = i * T
                own = i < NT_OWN
                for s in range(NSUB):
                    if i == 0 or s >= 2:
                        xs_, xb_ = xst[s % 2], xstb[s % 2]
                        S.dma("sp", xs_[:], x_d[r0 + s * 128:r0 + (s + 1) * 128, :], writes=[xb_])
                        prenorm_A(xs_, xb_, s, W)
                    prenorm_B(0, s, hT, hTb, W)
                if i == 0:
                    f2, f3 = W["f_sb"][2], W["f_sb"][3]
                    rows = [f2[0:1, q * 512:(q + 1) * 512] for q in range(4)] + \
                           [f3[0:1, q * 512:(q + 1) * 512] for q in range(4)]
                    vb = []
                    modgen1 = mod_vectors(list(range(NVA, NV0)), p1,
                                          views=(rows, W["f_sb"][1][:, 0:KC]), vbufs=vb)
                    ffn_gateup(hT, hTb, W, hook=lambda q: next(modgen1, None) if q < N_INTER else None)
                    for _ in modgen1:
                        pass
                    S.op("dve", lambda e: e.memset(W["f_sb"][1][:, 0:1], 0.0),
                         writes=vb + [W["fb"][1], W["fb"][2], W["fb"][3]])
                else:
                    ffn_gateup(hT, hTb, W)

                def epi1(s, r0=r0, own=own):
                    xs_, xb_ = xst[s % 2], xstb[s % 2]
                    S.dma("sp", xs_[:], x_d[r0 + s * 128:r0 + (s + 1) * 128, :], writes=[xb_])
                    yield
                    yield from postnorm_residual_g(s, W, gvecA, b_gA, xs_, xb_)
                    if own:
                        S.dma("sp", x1_d[r0 + s * 128:r0 + (s + 1) * 128, :], xs_[:], reads=[xb_])
                    yield from prenorm_A_g(xs_, xb_, s, W)

                vunits = []

                def vproj(s, r0=r0):
                    if not vunits:
                        for hq in range(2):
                            slot, sbf = ring.next("win", hold=hq)
                            vunits.append((slot[:, :].rearrange("p (a b) -> p a b", a=KC), sbf))
                    for jv, (wvv, sbfv) in enumerate(vunits):
                        bk = rot6()
                        for kc in range(KC):
                            S.op("pe", lambda e, kc=kc: e.matmul(
                                pa[bk][:, :], hT[:, kc, s * 128:(s + 1) * 128], wvv[:, kc, :],
                                start=(kc == 0), stop=(kc == KC - 1)),
                                reads=[sbfv, hTb[s]], writes=[pb[bk]])
                        vi = cnt["v"] % 2
                        cnt["v"] += 1
                        S.op("act", lambda e: e.copy(vst[vi][:], pa[bk][:, :]), reads=[pb[bk]], writes=[vstb[vi]])
                        S.dma("sp", v_d[KPAD + r0 + s * 128:KPAD + r0 + (s + 1) * 128, jv * 512:(jv + 1) * 512],
                              vst[vi][:], reads=[vstb[vi]])

                ffn_down(W, (epi1, lambda s: prenorm_B(1, s, hT, hTb, W), vproj))
                if i + 1 < n_tiles:
                    for s in range(2):
                        xs_, xb_ = xst[s % 2], xstb[s % 2]
                        S.dma("sp", xs_[:], x_d[r0 + T + s * 128:r0 + T + (s + 1) * 128, :], writes=[xb_])
                        prenorm_A(xs_, xb_, s, W)
                if dbg.get("p1_stop") == "c2":
                    break
                S.dma("sp", cosT[:], cos_d[:, r0:r0 + T], writes=[b_cos])
                S.dma("sp", sinT[:], sin_d[:, r0:r0 + T], writes=[b_sin])
                for j in (range(0, 4) if own else range(2, 4)):
                    slot, sbf = ring.next("win")
                    wv = slot[:, :].rearrange("p (a b) -> p a b", a=KC)
                    if j < 4:
                        for hh in range(4):
                            head = (j % 2) * 4 + hh
                            bk = rot6()
                            for kc in range(KC):
                                S.op("pe", lambda e, kc=kc: e.matmul(
                                    pa[bk][:, :], wv[:, kc, hh * 128:(hh + 1) * 128], hT[:, kc, :],
                                    start=(kc == 0), stop=(kc == KC - 1)),
                                    reads=[sbf] + hTb, writes=[pb[bk]])
                            S.op("dve", lambda e: e.tensor_tensor(tmpA[:], pa[bk][:, :], cosT[:], ALU.mult),
                                 reads=[pb[bk], b_cos], writes=[b_tA])
                            S.op("dve", lambda e: e.tensor_tensor(tmpB[0:64, :], pa[bk][64:128, :], sinT[0:64, :], ALU.mult),
                                 reads=[pb[bk], b_sin], writes=[b_tB])
                            S.op("dve", lambda e: e.tensor_tensor(tmpB[64:128, :], pa[bk][0:64, :], sinT[64:128, :], ALU.mult),
                                 reads=[pb[bk], b_sin], writes=[b_tB])
                            qi = cnt["q"] % 2
                            cnt["q"] += 1
                            S.op("dve", lambda e: e.tensor_tensor(qst[qi][:], tmpA[:], tmpB[:], ALU.add),
                                 reads=[b_tA, b_tB], writes=[qstb[qi]])
                            if j < 2:
                                dst = qT_d[head * 128:(head + 1) * 128, r0:r0 + T]
                            else:
                                dst = kT_d[head * 128:(head + 1) * 128, KPAD + r0:KPAD + r0 + T]
                            S.dma("sp", dst, qst[qi][:], reads=[qstb[qi]])
                    else:
                        for s in range(NSUB):
                            bk = rot6()
                            for kc in range(KC):
                                S.op("pe", lambda e, kc=kc: e.matmul(
                                    pa[bk][:, :], hT[:, kc, s * 128:(s + 1) * 128], wv[:, kc, :],
                                    start=(kc == 0), stop=(kc == KC - 1)),
                                    reads=[sbf, hTb[s]], writes=[pb[bk]])
                            vi = cnt["v"] % 2
                            cnt["v"] += 1
                            S.op("act", lambda e: e.copy(vst[vi][:], pa[bk][:, :]), reads=[pb[bk]], writes=[vstb[vi]])
                            S.dma("sp", v_d[KPAD + r0 + s * 128:KPAD + r0 + (s + 1) * 128, (j - 4) * 512:(j - 3) * 512],
                                  vst[vi][:], reads=[vstb[vi]])
                if i <= NT_OWN:
                    nt_c = T if own else 128
                    for j in range(4):
                        slot, sbf = ring.next("cv")
                        wcv = slot[:, 0:4096].rearrange("p (a b) -> p a b", a=KC)
                        wcg = slot[:, 4096:8192].rearrange("p (a b) -> p a b", a=KC)
                        for c2 in range(2):
                            cg = 2 * j + c2
                            iv = rot6()
                            ig = rot6()
                            for kc in range(KC):
                                S.op("pe", lambda e, kc=kc: e.matmul(
                                    pa[iv][:, 0:nt_c], wcv[:, kc, c2 * 128:(c2 + 1) * 128], hT[:, kc, 0:nt_c],
                                    start=(kc == 0), stop=(kc == KC - 1)),
                                    reads=[sbf] + hTb, writes=[pb[iv]])
                            for kc in range(KC):
                                S.op("pe", lambda e, kc=kc: e.matmul(
                                    pa[ig][:, 0:nt_c], wcg[:, kc, c2 * 128:(c2 + 1) * 128], hT[:, kc, 0:nt_c],
                                    start=(kc == 0), stop=(kc == KC - 1)),
                                    reads=[sbf] + hTb, writes=[pb[ig]])
                            sg = W["sgt"][cg % 2]
                            sgb = W["sgb"][cg % 2]
                            S.op("act", lambda e: e.activation(sg[:, 0:nt_c], pa[ig][:, 0:nt_c], AF.Sigmoid),
                                 reads=[pb[ig]], writes=[sgb])
                            ui = cnt["u"] % 2
                            cnt["u"] += 1
                            S.op("dve", lambda e: e.tensor_tensor(ust[ui][:, 0:nt_c], sg[:, 0:nt_c], pa[iv][:, 0:nt_c], ALU.mult),
                                 reads=[sgb, pb[iv]], writes=[ustb[ui]])
                            S.dma("sp", u_d[cg * 128:(cg + 1) * 128, UPAD + r0:UPAD + r0 + nt_c], ust[ui][:, 0:nt_c],
                                  reads=[ustb[ui]])
            S.barrier()

        if stop_after == "p1":
            return nc

        def rms_merge(allv, allb, gcol, row_base, sq, sqb, rstd, b_rstd, mg, mgb):
            for g in range(8):
                S.op("act", lambda e, g=g: e.activation(sq[g % 2], allv[:, g, :], AF.Square),
                     reads=[allb[g]], writes=[sqb[g % 2]])
                for tb in range(4):
                    S.op("pe", lambda e, g=g, tb=tb: e.matmul(
                        pa[tb][:, :], ones_f[:, :], sq[g % 2][:, tb * 512:(tb + 1) * 512],
                        start=(g == 0), stop=(g == 7)), reads=[sqb[g % 2], b_ones], writes=[pb[tb]])
            for tb in range(4):
                S.op("act", lambda e, tb=tb: e.activation(
                    rstd[:, tb * 512:(tb + 1) * 512], pa[tb][:, :], AF.Sqrt, bias=EPS, scale=1.0 / 1024),
                    reads=[pb[tb]], writes=[b_rstd])
            S.op("dve", lambda e: e.reciprocal(rstd, rstd), reads=[b_rstd], writes=[b_rstd])
            for g in range(8):
                S.op("dve", lambda e, g=g: e.scalar_tensor_tensor(
                    mg[g % 2], allv[:, g, :], gcol(g), rstd, ALU.mult, ALU.mult),
                    reads=[allb[g], b_rstd], writes=[mgb[g % 2]])
                S.dma("sp", merged_d[(row_base + g) * 128:(row_base + g + 1) * 128, :], mg[g % 2],
                      reads=[mgb[g % 2]])

        cpar = sb("cpar", [128, 40])
        b_cpar = S.buf("cpar")
        S.dma("sp", cpar[:], cpar_d[:, :], writes=[b_cpar])

        with ExitStack() as p3:
            call = sb("call", [128, 8 * TOK_OWN], F32, p3)
            callv = call[:, :].rearrange("p (g t) -> p g t", g=8)
            callb = S.bufs(8, "call")
            utb16 = [sb("utb%d" % i, [128, TOK_OWN + 32], BF16, p3) for i in range(2)]
            utbb = S.bufs(2, "utb")
            ut0 = sb("ut0", [128, TOK_OWN], F32, p3)
            ut = [ut0, ut0]
            ut0b = S.buf("ut")
            utb = [ut0b, ut0b]
            dg = [sb("dg%d" % i, [128, 31 * 128], BF16, p3) for i in range(2)]
            dgb = S.bufs(2, "dg")
            cw = sb("cw", [128, 8 * 31], F32, p3)
            b_cw = S.buf("cw")
            S.dma("sp", cw[:], convw_d[:, :], writes=[b_cw])
            modgen = mod_vectors(list(range(NV0, 9)), p3)
            for cg in range(8):
                u_, ub_ = utb16[cg % 2], utbb[cg % 2]
                S.dma("sp", u_[:], u_d[cg * 128:(cg + 1) * 128, 0:TOK_OWN + 32], writes=[ub_])
                dg_, dgb_ = dg[cg % 2], dgb[cg % 2]
                for j in range(31):
                    wj = cw[:, cg * 31 + j:cg * 31 + j + 1]
                    S.op("dve", lambda e, wj=wj, j=j: e.tensor_scalar(
                        dg_[:, j * 128:(j + 1) * 128], ident[:, :], wj, None, ALU.mult),
                        reads=[b_ident, b_cw], writes=[dgb_], selfsync=False)
                for tb in range(4):
                    bk = rot6()
                    for j in range(31):
                        S.op("pe", lambda e, j=j, tb=tb: e.matmul(
                            pa[bk][:, :], dg_[:, j * 128:(j + 1) * 128],
                            u_[:, 1 + j + tb * 512:1 + j + (tb + 1) * 512],
                            start=(j == 0), stop=(j == 30)), reads=[dgb_, ub_], writes=[pb[bk]])
                    dst = callv[:, cg, tb * 512:(tb + 1) * 512]
                    if tb % 2 == 0:
                        S.op("act", lambda e, dst=dst: e.activation(
                            dst, pa[bk][:, :], AF.Identity, bias=cpar[:, cg:cg + 1], scale=1.0),
                            reads=[pb[bk], b_cpar], writes=[callb[cg]])
                    else:
                        S.op("dve", lambda e, dst=dst: e.tensor_scalar(
                            dst, pa[bk][:, :], cpar[:, cg:cg + 1], None, ALU.add),
                            reads=[pb[bk], b_cpar], writes=[callb[cg]])
                for _ in range(2):
                    next(modgen, None)
            for _ in modgen:
                pass
            sqa = [ut[i][:, :] for i in range(2)]
            sqab = utb
            mean = sb("mean", [128, TOK_OWN], F32, p3)
            b_mean = S.buf("mean")
            lrs = sb("lrs", [128, TOK_OWN], F32, p3)
            b_lrs = S.buf("lrs")
            for cg in range(8):
                for tb in range(4):
                    S.op("pe", lambda e, cg=cg, tb=tb: e.matmul(
                        pa[tb][:, :], ones_f[:, :], callv[:, cg, tb * 512:(tb + 1) * 512],
                        start=(cg == 0), stop=(cg == 7)), reads=[callb[cg], b_ones], writes=[pb[tb]])
            for tb in range(4):
                S.op("act", lambda e, tb=tb: e.activation(
                    mean[:, tb * 512:(tb + 1) * 512], pa[tb][:, :], AF.Copy, scale=1.0 / 1024),
                    reads=[pb[tb]], writes=[b_mean])
            for cg in range(8):
                S.op("act", lambda e, cg=cg: e.activation(sqa[cg % 2], callv[:, cg, :], AF.Square),
                     reads=[callb[cg]], writes=[sqab[cg % 2]])
                for tb in range(4):
                    S.op("pe", lambda e, cg=cg, tb=tb: e.matmul(
                        pa[tb][:, :], ones_f[:, :], sqa[cg % 2][:, tb * 512:(tb + 1) * 512],
                        start=(cg == 0), stop=(cg == 7)), reads=[sqab[cg % 2], b_ones], writes=[pb[tb]])
            S.op("dve", lambda e: e.tensor_tensor(lrs[:], mean[:], mean[:], ALU.mult),
                 reads=[b_mean], writes=[b_lrs])
            for tb in range(4):
                S.op("dve", lambda e, tb=tb: e.scalar_tensor_tensor(
                    lrs[:, tb * 512:(tb + 1) * 512], pa[tb][:, :], 1.0 / 1024, lrs[:, tb * 512:(tb + 1) * 512],
                    ALU.mult, ALU.subtract), reads=[pb[tb], b_lrs], writes=[b_lrs])
            S.op("act", lambda e: e.activation(lrs[:], lrs[:], AF.Sqrt, bias=EPS, scale=1.0),
                 reads=[b_lrs], writes=[b_lrs])
            S.op("dve", lambda e: e.reciprocal(lrs[:], lrs[:]), reads=[b_lrs], writes=[b_lrs])
            for cg in range(8):
                acc = callv[:, cg, :]
                S.op("dve", lambda e, acc=acc: e.tensor_tensor(acc, acc, mean[:], ALU.subtract),
                     reads=[b_mean], writes=[callb[cg]])
                S.op("dve", lambda e, acc=acc: e.tensor_tensor(acc, acc, lrs[:], ALU.mult),
                     reads=[b_lrs], writes=[callb[cg]])
                S.op("act", lambda e, acc=acc, cg=cg: e.activation(
                    acc, acc, AF.Silu, bias=cpar[:, 16 + cg:16 + cg + 1], scale=cpar[:, 8 + cg:8 + cg + 1]),
                    reads=[b_cpar], writes=[callb[cg]])
            rms_merge(callv, callb, lambda g: cpar[:, 24 + g:24 + g + 1], 8, sqa, sqab, lrs[:, :], b_lrs,
                      [dg[0][:, 0:TOK_OWN], dg[1][:, 0:TOK_OWN]], dgb)
            S.barrier()

        if stop_after == "p3":
            return nc

        with ExitStack() as p2:
            aall = sb("aall", [128, 8 * TOK_OWN], F32, p2)
            aallv = aall[:, :].rearrange("p (g t) -> p g t", g=8)
            aallb = S.bufs(8, "aall")
            qh = sb("qh", [128, TOK_OWN], BF16, p2)
            kh = sb("kh", [128, KPAD + TOK_ALL], BF16, p2)
            b_qh, b_kh = S.bufs(2, "qk")
            NVT = 69
            vt = [sb("vt%d" % i, [128, NVT * 128], BF16, p2) for i in range(2)]
            vtb = S.bufs(2, "vt")
            accn = sb("accn", [128, TOK_OWN], F32, p2)
            accd = sb("accd", [128, TOK_OWN], F32, p2)
            b_an, b_ad = S.bufs(2, "acc")
            ptl = [sb("pt%d" % i, [128, 256], BF16, p2) for i in range(3)]
            ptb_ = S.bufs(3, "pt")
            maskb = sb("maskb", [128, 512], BF16, p2)
            b_mask = S.buf("mask")
            ones_b = sb("ones_b", [128, 128], BF16, p2)
            b_onesb = S.buf("onesb")
            S.dma("sp", maskb[:], maskb_d[:, :], writes=[b_mask])
            S.op("dve", lambda e: e.memset(ones_b[:], 1.0), writes=[b_onesb])
            b_mask.const = True
            b_onesb.const = True
            PATS = ((1, 16), (4, 4), (16, 1))
            nchunk = 0
            vbase = {}

            def load_v(h):
                vth, vthb = vt[h % 2], vtb[h % 2]
                vidx = 0
                for (d, nblk) in PATS:
                    nch = nblk + 1
                    base = KPAD - 64 * d
                    rows = v_d[base:base + 128 * d * nch, h * 128:(h + 1) * 128].rearrange(
                        "(c a dd) e -> a c dd e", a=128, dd=d)
                    for r in range(d):
                        dst = vth[:, vidx * 128:(vidx + nch) * 128].rearrange("p (c e) -> p c e", c=nch)
                        S.dma("sp", dst, rows[:, :, r, :], writes=[vthb])
                        vbase[(d, r)] = vidx
                        vidx += nch

            load_v(0)
            for h in range(HEADS):
                S.dma("sp", qh[:], qT_d[h * 128:(h + 1) * 128, :], writes=[b_qh])
                S.dma("sp", kh[:], kT_d[h * 128:(h + 1) * 128, :], writes=[b_kh])
                if h + 1 < HEADS:
                    load_v(h + 1)
                vth, vthb = vt[h % 2], vtb[h % 2]
                chunks = []
                for pi, (d, nblk) in enumerate(PATS):
                    for r in range(d):
                        for c in range(nblk + 1):
                            chunks.append((pi, d, nblk, r, c))

                def emit_scores(ci):
                    pi, d, nblk, r, c = chunks[ci]
                    qv = qh[:, :].rearrange("p (m dd) -> p dd m", dd=d)
                    kv = kh[:, :].rearrange("p (m dd) -> p dd m", dd=d)
                    q0 = max(c - 1, 0) * 128
                    q1 = min(c + 1, nblk) * 128
                    ncol = q1 - q0
                    mc0 = (1 if c == 0 else 0) * 256 + (128 if c == 0 else 0)
                    kst = KPAD // d - 64 + 128 * c
                    g = nchunk0 + ci
                    bst = g % 2
                    pti = g % 3
                    S.op("pe", lambda e: e.matmul(
                        pa[bst][:, 0:ncol], kv[:, r, kst:kst + 128], qv[:, r, q0:q1],
                        start=True, stop=False), reads=[b_kh, b_qh], writes=[pb[bst]])
                    S.op("pe", lambda e: e.matmul(
                        pa[bst][:, 0:ncol], ident[:, :], maskb[:, mc0:mc0 + ncol],
                        start=False, stop=True), reads=[b_ident, b_mask], writes=[pb[bst]])
                    S.op("act", lambda e: e.activation(
                        ptl[pti][:, 0:ncol], pa[bst][:, 0:ncol], AF.Exp, scale=SCALE),
                        reads=[pb[bst]], writes=[ptb_[pti]])

                def emit_pv(ci):
                    pi, d, nblk, r, c = chunks[ci]
                    anv = accn[:, :].rearrange("p (m dd) -> p dd m", dd=d)
                    adv = accd[:, :].rearrange("p (m dd) -> p dd m", dd=d)
                    q0 = max(c - 1, 0) * 128
                    g = nchunk0 + ci
                    pti = g % 3
                    vtile = vth[:, (vbase[(d, r)] + c) * 128:(vbase[(d, r)] + c + 1) * 128]
                    for jb in (c - 1, c):
                        if jb < 0 or jb >= nblk:
                            continue
                        pc0 = jb * 128 - q0
                        first = (jb == c)
                        bn = 2 + jb % 2
                        bd = 4 + jb % 2
                        S.op("pe", lambda e: e.matmul(
                            pa[bn][:, 0:128], vtile, ptl[pti][:, pc0:pc0 + 128],
                            start=first, stop=not first), reads=[vthb, ptb_[pti]], writes=[pb[bn]])
                        S.op("pe", lambda e: e.matmul(
                            pa[bd][:, 0:128], ones_b[:, :], ptl[pti][:, pc0:pc0 + 128],
                            start=first, stop=not first), reads=[b_onesb, ptb_[pti]], writes=[pb[bd]])
                    if c >= 1:
                        jb = c - 1
                        bn = 2 + jb % 2
                        bd = 4 + jb % 2
                        dn_ = anv[:, r, jb * 128:(jb + 1) * 128]
                        dd_ = adv[:, r, jb * 128:(jb + 1) * 128]
                        if pi == 0:
                            S.op("act", lambda e: e.copy(dn_, pa[bn][:, 0:128]),
                                 reads=[pb[bn]], writes=[b_an], selfsync=False)
                            S.op("act", lambda e: e.copy(dd_, pa[bd][:, 0:128]),
                                 reads=[pb[bd]], writes=[b_ad], selfsync=False)
                        else:
                            S.op("dve", lambda e: e.tensor_tensor(dn_, dn_, pa[bn][:, 0:128], ALU.add),
                                 reads=[pb[bn]], writes=[b_an], selfsync=False)
                            S.op("dve", lambda e: e.tensor_tensor(dd_, dd_, pa[bd][:, 0:128], ALU.add),
                                 reads=[pb[bd]], writes=[b_ad], selfsync=False)

                nchunk0 = nchunk
                emit_scores(0)
                for ci in range(len(chunks)):
                    if ci + 1 < len(chunks):
                        emit_scores(ci + 1)
                    emit_pv(ci)
                nchunk += len(chunks)
                S.op("dve", lambda e: e.reciprocal(accd[:], accd[:]), reads=[b_ad], writes=[b_ad])
                S.op("dve", lambda e, h=h: e.tensor_tensor(aallv[:, h, :], accn[:], accd[:], ALU.mult),
                     reads=[b_an, b_ad], writes=[aallb[h]])
            rms_merge(aallv, aallb, lambda g: cpar[:, 32 + g:32 + g + 1], 0,
                      [accn[:, :], accd[:, :]], [b_an, b_ad],
                      vt[0][:, 0:2 * TOK_OWN].bitcast(F32), vtb[0],
                      [vt[1][:, 0:TOK_OWN], vt[1][:, TOK_OWN:2 * TOK_OWN]], [vtb[1], vtb[1]])
            S.barrier()

        if stop_after == "p2":
            return nc

        with ExitStack() as p4:
            W = {}
            hT, hTb, xst, xstb = alloc_ffn_work(W, p4, "b")
            gvecB = sb("gvecB", [128, D], F32, p4)
            b_gB = S.buf("gB")
            mtv = W["actT"][:, 0:KC, :]
            mtb = W["actb"][0:KC]
            x2b = S.bufs(NSUB, "x2d")
            S.dma("sp", gvecB[:], g2row_d[0:1, :].partition_broadcast(128), writes=[b_gB])
            S.dma("sp", gvecA[:], g3row_d[0:1, :].partition_broadcast(128), writes=[b_gA])
            merged_v = merged_d.rearrange("(c p) t -> p c t", p=128)
            for i in range(dbg.get("n_tiles4", NT_OWN)):
                r0 = i * T
                S.dma("sp", mtv, merged_v[:, :, r0:r0 + T], writes=mtb)

                def epi_w(s, r0=r0):
                    xs_, xb_ = xst[s % 2], xstb[s % 2]
                    S.dma("sp", xs_[:], x1_d[r0 + s * 128:r0 + (s + 1) * 128, :], writes=[xb_])
                    yield
                    yield from postnorm_residual_g(s, W, gvecB, b_gB, xs_, xb_)
                    S.dma("sp", x2_d[r0 + s * 128:r0 + (s + 1) * 128, :], xs_[:], reads=[xb_], writes=[x2b[s]])
                    yield from prenorm_A_g(xs_, xb_, s, W)

                proj_down(W, mtv, mtb, KC, "wout", ((0, KC),),
                          (epi_w, lambda s: prenorm_B(2, s, hT, hTb, W), lambda s: None))
                ffn_gateup(hT, hTb, W)

                def epi2(s, r0=r0):
                    xs_, xb_ = xst[s % 2], xstb[s % 2]
                    S.dma("sp", xs_[:], x2_d[r0 + s * 128:r0 + (s + 1) * 128, :], reads=[x2b[s]], writes=[xb_])
                    yield
                    yield from postnorm_residual_g(s, W, gvecA, b_gA, xs_, xb_)
                    S.dma("sp", out_d[r0 + s * 128:r0 + (s + 1) * 128, :], xs_[:], reads=[xb_])
                    yield

                ffn_down(W, (epi2, lambda s: None, lambda s: None))
            S.barrier()
    return nc


def _cols(v, n):
    return np.ascontiguousarray(np.asarray(v, np.float32).reshape(n, 128).T)


def prep_core_inputs(core, inp, shared):
    b = core // 2
    rev = core % 2 == 1
    x = inp["x"][b]
    S = x.shape[0]
    if not rev:
        gpos = np.arange(0, TOK_ALL)
    else:
        gpos = S - 1 - np.arange(0, TOK_ALL)
    m = dict(shared)
    m["x"] = np.ascontiguousarray(x[gpos])
    m["c_col"] = _cols(inp["c"][b], KC)
    cw = np.asarray(inp["conv_w"][0], np.float32)
    if rev:
        cw = cw[::-1]
    m["conv_w_cols"] = np.ascontiguousarray(
        cw.T.reshape(8, 128, 31).transpose(1, 0, 2).reshape(128, 8 * 31))
    half = 64
    inv_freq = (10000.0 ** (-np.arange(half, dtype=np.float32) / half)).astype(np.float32)
    ang = gpos.astype(np.float32)[None, :] * inv_freq[:, None]
    cos = np.cos(ang).astype(np.float32)
    sin = np.sin(ang).astype(np.float32)
    m["rope_cos"] = np.ascontiguousarray(np.concatenate([cos, cos], 0))
    m["rope_sin"] = np.ascontiguousarray(np.concatenate([-sin, sin], 0))
    return m


def prep_shared(inp):
    sh = {}
    for k in ("w_ada", "ffn1_w_gate", "ffn1_w_up", "ffn1_w_down", "ffn2_w_gate", "ffn2_w_up",
              "ffn2_w_down", "w_in", "w_out"):
        sh[k] = np.ascontiguousarray(np.asarray(inp[k], np.float32)[0])
    sh["b_ada"] = np.ascontiguousarray(np.asarray(inp["b_ada"], np.float32).reshape(1, -1))
    sh["pre_g_cols"] = np.ascontiguousarray(np.concatenate(
        [_cols(inp[k][0], KC) for k in ("ffn1_pre_g", "mix_pre_g", "ffn2_pre_g")], 1))
    sh["post_g_rows"] = np.ascontiguousarray(np.concatenate(
        [np.asarray(inp[k][0], np.float32) for k in ("ffn1_post_g", "mix_post_g", "ffn2_post_g")])[None, :])
    sh["cpar_cols"] = np.ascontiguousarray(np.concatenate(
        [_cols(inp[k][0], 8) for k in ("conv_b", "conv_ln_g", "conv_ln_b", "conv_out_g", "attn_out_g")], 1))
    sh["ident"] = np.eye(128, dtype=np.float32).astype(ml_dtypes.bfloat16)
    a = np.arange(128)[:, None]
    bq = np.arange(128)[None, :]
    NEG = -30000.0
    mB = np.where(a <= bq, 0.0, NEG)
    mA = np.where(a >= bq, 0.0, NEG)
    normal = np.concatenate([mB, mA], 1)
    edge = normal.copy()
    edge[:64, :] = NEG
    sh["maskb"] = np.ascontiguousarray(np.concatenate([normal, edge], 1).astype(np.float32)).astype(ml_dtypes.bfloat16)
    return sh


def kernel(**inputs):
    inp = {k: np.asarray(v) for k, v in inputs.items()}
    B, S, Dm = inp["x"].shape
    shared = prep_shared(inp)
    in_maps = [prep_core_inputs(c, inp, shared) for c in range(8)]
    nc = build_nc()
    res = run_bass_kernel_spmd(nc, in_maps, core_ids=list(range(8)))
    out = np.empty((B, S, Dm), np.float32)
    for c in range(8):
        o = res.results[c]["out"]
        b = c // 2
        if c % 2 == 0:
            out[b, 0:TOK_OWN] = o
        else:
            out[b, S - 1 - np.arange(TOK_OWN)] = o
    return out
```

```python
import numpy as np
import ml_dtypes
from contextlib import ExitStack
import concourse.bass as bass
import concourse.mybir as mybir
from concourse.bass_utils import run_bass_kernel_spmd

F32 = mybir.dt.float32
BF16 = mybir.dt.bfloat16
ALU = mybir.AluOpType
AF = mybir.ActivationFunctionType
AX = mybir.AxisListType

D = 2048
KC = 16
FF = 5632
FC = 44
T = 512
NSUB = 4
TOK_ALL = 3072
TOK_OWN = 2048
NT_ALL = TOK_ALL // T
NT_OWN = TOK_OWN // T
EPS = 1e-6
UPAD = 16
UW = UPAD + 2560
HEADS = 8
SCALE = 128 ** -0.5


class Buf:
    __slots__ = ("name", "w", "r", "dsem", "dcnt", "const")

    def __init__(self, name):
        self.name = name
        self.w = None
        self.r = {}
        self.dsem = None
        self.dcnt = 0
        self.const = False


class Sched:
    ENG = ("pe", "act", "dve", "pool", "sp")

    def __init__(self, nc, stack):
        self.nc = nc
        self.stack = stack
        self.h = {"pe": nc.tensor, "act": nc.scalar, "dve": nc.vector,
                  "pool": nc.gpsimd, "sp": nc.sync}
        self.sem = {}
        self.cnt = {}
        self.semobj = {}
        for e in self.ENG:
            s = stack.enter_context(nc.semaphore("s_" + e))
            self.sem[e] = s
            self.cnt[e] = 0
            self.semobj[e] = s
        self.seen = {e: {} for e in self.ENG}
        self.pending = {e: False for e in self.ENG}
        self.nbuf = 0
        self.ndsem = 0
        self.dbufs = []
        self._bykey = {}

    def buf(self, name=None):
        self.nbuf += 1
        return Buf(name or ("b%d" % self.nbuf))

    def bufs(self, n, name="b"):
        return [self.buf("%s%d" % (name, i)) for i in range(n)]

    def _dsem(self, b):
        key = ("d", id(b))
        if b.dsem is None:
            self.ndsem += 1
            b.dsem = self.stack.enter_context(self.nc.semaphore("d%d" % self.ndsem))
            self.semobj[key] = b.dsem
            self.dbufs.append(b)
            self._bykey[key] = b
        return key

    def _wait(self, e, key, count):
        if key in self.cnt:
            assert count <= self.cnt[key], ("wait on an unclosed run", e, key, count, self.cnt[key])
        if self.seen[e].get(key, 0) >= count:
            return
        self.seen[e][key] = count
        self.h[e].wait_ge(self.semobj[key], count)

    def _deps(self, e, reads, writes, selfsync, skip_key=None):
        deps = {}

        def add(k, c):
            if isinstance(k, tuple):
                c = max(c, self._bykey[k].dcnt)
            if deps.get(k, 0) < c:
                deps[k] = c
        for b in reads:
            if b.w is not None:
                add(*b.w)
        for b in writes:
            if b.w is not None and b.w[0] != skip_key:
                add(*b.w)
            for k, c in b.r.items():
                add(k, c)
        for k, c in deps.items():
            if k == e:
                if not selfsync or e == "pe":
                    continue
            self._wait(e, k, c)

    def _record(self, rec, reads, writes):
        k, c = rec
        for b in reads:
            if not b.const:
                if b.r.get(k, 0) < c:
                    b.r[k] = c
        for b in writes:
            b.w = rec
            b.r = {}

    def op(self, e, fn, reads=(), writes=(), selfsync=True, inc=True):
        self._deps(e, reads, writes, selfsync)
        ins = fn(self.h[e])
        if inc:
            self.cnt[e] += 1
            ins.then_inc(self.sem[e], 1)
            self.pending[e] = False
            self._record((e, self.cnt[e]), reads, writes)
        else:
            self.pending[e] = True
            self._record((e, self.cnt[e] + 1), reads, writes)
        return ins

    def dma(self, q, out, in_, reads=(), writes=(), sembuf=None, **kw):
        sb = sembuf or (writes[0] if writes else reads[0])
        key = self._dsem(sb)
        self._deps(q, reads, writes, False, skip_key=key)
        ins = self.h[q].dma_start(out=out, in_=in_, **kw)
        sb.dcnt += 16
        ins.then_inc(sb.dsem, 16)
        self._record((key, sb.dcnt), reads, writes)
        return ins

    def wait_all(self, e, bufs):
        self._deps(e, [], bufs, False)

    def barrier(self):
        assert not any(self.pending.values()), self.pending
        for e in self.ENG:
            for k in self.ENG:
                if k != e and self.cnt[k] > 0:
                    self._wait(e, k, self.cnt[k])
            for b in self.dbufs:
                if b.dcnt > 0:
                    self._wait(e, ("d", id(b)), b.dcnt)


class Unit:
    __slots__ = ("tag", "parts", "cache", "ncols")

    def __init__(self, tag, parts, cache=None):
        self.tag = tag
        self.parts = parts
        self.cache = cache
        self.ncols = max(off + a * b for (off, a, b, _) in parts)


class WRing:
    def __init__(self, S, nc, stack, nslots, plan):
        self.S = S
        self.n = nslots
        self.slots = [stack.enter_context(nc.sbuf_tensor("wr%d" % i, [128, 8192], BF16))
                      for i in range(nslots)]
        self.bufs = S.bufs(nslots, "wr")
        self.plan = plan
        self.issued = 0
        self.cur = 0
        self.cache_d = None
        self.cache_rec = {}

    def next(self, tag, hold=0):
        k = self.cur
        assert self.plan[k].tag == tag, (k, self.plan[k].tag, tag)
        lim = min(len(self.plan), k - hold + self.n)
        S = self.S
        while self.issued < lim:
            j = self.issued
            slot = self.slots[j % self.n]
            b = self.bufs[j % self.n]
            u = self.plan[j]
            if j >= self.n:
                old = self.plan[j - self.n]
                if old.cache is not None and old.cache[0] == "store":
                    idx = old.cache[1]
                    S.dma("pool", self.cache_d[idx * 128:(idx + 1) * 128, 0:old.ncols], slot[:, 0:old.ncols],
                          reads=[b], sembuf=b)
                    self.cache_rec[idx] = (("d", id(b)), b.dcnt)
            if u.cache is not None and u.cache[0] == "load":
                idx = u.cache[1]
                key, cnt = self.cache_rec[idx]
                S._wait("pool", key, cnt)
                S.dma("pool", slot[:, 0:u.ncols], self.cache_d[idx * 128:(idx + 1) * 128, 0:u.ncols], writes=[b])
            else:
                for (off, a, bb, src) in u.parts:
                    dst = slot[:, off:off + a * bb].rearrange("p (a b) -> p a b", a=a)
                    S.dma("pool", dst, src, writes=[b])
            self.issued += 1
        self.cur += 1
        return self.slots[k % self.n], self.bufs[k % self.n]


FGROUPS = ((0, 16), (16, 32), (32, 44))


def ffn_units(wg_v, wu_v, wd_v):
    us = []
    for u in range(FC // 2):
        us.append(Unit("gu", [(0, KC, 256, wg_v[:, :, u * 256:(u + 1) * 256]),
                              (4096, KC, 256, wu_v[:, :, u * 256:(u + 1) * 256])]))
    for n in range(4):
        for (f0, f1) in FGROUPS:
            us.append(Unit("dn", [(0, f1 - f0, 512, wd_v[:, f0:f1, n * 512:(n + 1) * 512])]))
    return us


def build_nc(dbg=None):
    dbg = dbg or {}
    n_tiles = dbg.get("n_tiles", NT_ALL)
    stop_after = dbg.get("stop_after", "all")

    nc = bass.Bass("TRN2", target_bir_lowering=False)

    def din(name, shape, dt=F32):
        return nc.dram_tensor(name, shape, dt, kind="ExternalInput").ap()

    def dscr(name, shape, dt=F32):
        kind = "ExternalOutput" if dbg.get("expose") else "Internal"
        return nc.dram_tensor(name, shape, dt, kind=kind).ap()

    x_d = din("x", [TOK_ALL, D])
    ccol_d = din("c_col", [128, KC])
    wada_d = din("w_ada", [D, 9 * D])
    bada_d = din("b_ada", [1, 9 * D])
    pregc_d = din("pre_g_cols", [128, 3 * KC])
    postg_d = din("post_g_rows", [1, 3 * D])
    f1g_d = din("ffn1_w_gate", [D, FF])
    f1u_d = din("ffn1_w_up", [D, FF])
    f1d_d = din("ffn1_w_down", [FF, D])
    f2g_d = din("ffn2_w_gate", [D, FF])
    f2u_d = din("ffn2_w_up", [D, FF])
    f2d_d = din("ffn2_w_down", [FF, D])
    win_d = din("w_in", [D, 5120])
    wout_d = din("w_out", [D, D])
    convw_d = din("conv_w_cols", [128, 8 * 31])
    cpar_d = din("cpar_cols", [128, 5 * 8])
    cos_d = din("rope_cos", [128, TOK_ALL])
    sin_d = din("rope_sin", [128, TOK_ALL])
    ident_d = din("ident", [128, 128], BF16)
    maskb_d = din("maskb", [128, 2 * 256], BF16)
    out_d = nc.dram_tensor("out", [TOK_OWN, D], F32, kind="ExternalOutput").ap()

    x1_d = dscr("x1_s", [TOK_OWN, D])
    qT_d = dscr("qT_s", [HEADS * 128, TOK_OWN], BF16)
    KPAD = 1024
    kT_d = dscr("kT_s", [HEADS * 128, KPAD + TOK_ALL], BF16)
    v_d = dscr("v_s", [KPAD + TOK_ALL, 1024], BF16)
    merged_d = dscr("merged_s", [16 * 128, TOK_OWN], BF16)
    x2_d = dscr("x2_s", [TOK_OWN, D])
    u_d = dscr("u_s", [8 * 128, UW], BF16)
    g2row_d = dscr("g2row_s", [1, D])
    g3row_d = dscr("g3row_s", [1, D])

    wada_v = wada_d.rearrange("(kc p) f -> p kc f", p=128)
    win_v = win_d.rearrange("(kc p) f -> p kc f", p=128)
    wout_v = wout_d.rearrange("(kc p) f -> p kc f", p=128)
    f1 = (f1g_d.rearrange("(kc p) f -> p kc f", p=128), f1u_d.rearrange("(kc p) f -> p kc f", p=128),
          f1d_d.rearrange("(fc p) d -> p fc d", p=128))
    f2 = (f2g_d.rearrange("(kc p) f -> p kc f", p=128), f2u_d.rearrange("(kc p) f -> p kc f", p=128),
          f2d_d.rearrange("(fc p) d -> p fc d", p=128))

    def win_unit(j):
        return Unit("win", [(0, KC, 512, win_v[:, :, j * 512:(j + 1) * 512])])

    def conv_unit(j):
        return Unit("cv", [(0, KC, 256, win_v[:, :, 3072 + 256 * j:3072 + 256 * (j + 1)]),
                           (4096, KC, 256, win_v[:, :, 4096 + 256 * j:4096 + 256 * (j + 1)])])

    plan = []
    NV0 = 5
    NVA = 2
    for j in range(4 * NVA):
        plan.append(Unit("ada", [(0, KC, 512, wada_v[:, :, j * 512:(j + 1) * 512])]))
    N_INTER = 4 * (NV0 - NVA)
    ncache = [0]
    cmap = {}

    def cached(units, group, first):
        out = []
        for q, u in enumerate(units):
            key = (group, q)
            if first:
                cmap[key] = ncache[0]
                ncache[0] += 1
                u.cache = ("store", cmap[key])
            else:
                u.cache = ("load", cmap[key])
            out.append(u)
        return out

    use_cache = dbg.get("cache", False)
    if stop_after != "p0":
        for i in range(n_tiles):
            fu = ffn_units(*f1)
            wu = [win_unit(j) for j in range(6)]
            cu = [conv_unit(j) for j in range(4)]
            if use_cache:
                fu = cached(fu, "f1", i == 0)
                wu = cached(wu, "win", i == 0)
                cu = cached(cu, "cv", i == 0)
            if i == 0:
                fu2 = []
                for q, u_ in enumerate(fu):
                    fu2.append(u_)
                    if q < N_INTER:
                        jj = 4 * NVA + q
                        fu2.append(Unit("ada", [(0, KC, 512, wada_v[:, :, jj * 512:(jj + 1) * 512])]))
                fu = fu2
            plan += fu
            plan += (wu[4:6] + wu[0:4]) if i < NT_OWN else (wu[4:6] + wu[2:4])
            if i <= NT_OWN:
                plan += cu
        for j in range(4 * NV0, 36):
            plan.append(Unit("ada", [(0, KC, 512, wada_v[:, :, j * 512:(j + 1) * 512])]))
    if stop_after == "all":
        for i in range(NT_OWN):
            ou = [Unit("wout", [(0, KC, 512, wout_v[:, :, n * 512:(n + 1) * 512])]) for n in range(4)]
            fu = ffn_units(*f2)
            if use_cache:
                ou = cached(ou, "wo", i == 0)
                fu = cached(fu, "f2", i == 0)
            plan += ou
            plan += fu
    cache_d = dscr("wcache_s", [max(ncache[0], 1) * 128, 8192], BF16)

    with ExitStack() as st:
        S = Sched(nc, st)

        def sb(name, shape, dt=F32, stack=st):
            return stack.enter_context(nc.sbuf_tensor("s_" + name, shape, dt))

        pa = [st.enter_context(nc.psum_tensor("pa%d" % i, [128, 512], F32)) for i in range(8)]
        pb = S.bufs(8, "pb")
        rot = {"acc": 0, "tr": 0}

        def rot6():
            i = rot["acc"]
            rot["acc"] = (i + 1) % 6
            return i

        def rot2():
            i = 6 + rot["tr"]
            rot["tr"] = (rot["tr"] + 1) % 2
            return i

        ident = sb("ident", [128, 128], BF16)
        ones_f = sb("ones_f", [128, 128])
        modcols = sb("modcols", [128, 6 * KC])
        pregc = sb("pregc", [128, 3 * KC])
        gvecA = sb("gvecA", [128, D])
        b_ident, b_ones, b_mod, b_pregc, b_gA = S.bufs(5, "c")

        ring = WRing(S, nc, st, dbg.get("nslots", 4), plan)
        ring.cache_d = cache_d

        S.dma("sp", ident[:], ident_d[:, :], writes=[b_ident])
        S.dma("sp", pregc[:], pregc_d[:, :], writes=[b_pregc])
        S.op("dve", lambda e: e.memset(ones_f[:], 1.0), writes=[b_ones])

        ccol = sb("ccol", [128, KC], F32)
        cact = sb("cact", [128, KC], BF16)
        b_ccol, b_cact = S.bufs(2, "p0")
        S.dma("sp", ccol[:], ccol_d[:, :], writes=[b_ccol])
        S.op("act", lambda e: e.activation(cact[:], ccol[:], AF.Silu), reads=[b_ccol], writes=[b_cact])

        def mod_vectors(vs, stk, views=None, vbufs=None):
            tg = "m%d" % vs[0]
            if views is None:
                mrow = [sb("mrow%s%d" % (tg, i), [1, 512], F32, stk) for i in range(2)]
                brow = [sb("brow%s%d" % (tg, i), [1, 512], F32, stk) for i in range(2)]
                prow = [sb("prow%s%d" % (tg, i), [1, 512], F32, stk) for i in range(2)]
                grow = [sb("grow%s%d" % (tg, i), [1, 512], F32, stk) for i in range(2)]
                tmpc = sb("tmpc" + tg, [128, KC], F32, stk)
            else:
                rows, tmpc = views
                mrow, brow, prow, grow = rows[0:2], rows[2:4], rows[4:6], rows[6:8]
            b_mrow = S.bufs(2, "mrow")
            b_brow = S.bufs(2, "brow")
            b_prow = S.bufs(2, "prow")
            b_grow = S.bufs(2, "grow")
            b_tmpc = S.buf("tmpc")
            if vbufs is not None:
                vbufs.extend(b_mrow + b_brow + b_prow + b_grow + [b_tmpc])
            np_ = 0
            for v in vs:
                k = v // 3
                kind = v % 3
                bkc = (6 + v % 2) if kind in (0, 1) else None
                for j in range(4):
                    pi = np_ % 2
                    np_ += 1
                    c0 = v * D + j * 512
                    S.dma("sp", brow[pi][:], bada_d[0:1, c0:c0 + 512], writes=[b_brow[pi]])
                    slot, sbf = ring.next("ada")
                    wv = slot[:, :].rearrange("p (a b) -> p a b", a=KC)
                    bk = rot6()
                    for kc in range(KC):
                        S.op("pe", lambda e, kc=kc, bk=bk: e.matmul(
                            pa[bk][0:1, :], cact[:, kc:kc + 1], wv[:, kc, :],
                            start=(kc == 0), stop=False),
                            reads=[b_cact, sbf], writes=[pb[bk]])
                    S.op("pe", lambda e, bk=bk, pi=pi: e.matmul(
                        pa[bk][0:1, :], ones_f[0:1, 0:1], brow[pi][0:1, :], start=False, stop=True),
                        reads=[b_ones, b_brow[pi]], writes=[pb[bk]])
                    S.op("act", lambda e, bk=bk, pi=pi: e.copy(mrow[pi][0:1, :], pa[bk][0:1, :]),
                         reads=[pb[bk]], writes=[b_mrow[pi]])
                    if kind in (0, 1):
                        for q in range(4):
                            kc = j * 4 + q
                            S.op("pe", lambda e, kc=kc, q=q, pi=pi: e.matmul(
                                pa[bkc][:, kc:kc + 1], mrow[pi][0:1, q * 128:(q + 1) * 128], ones_f[0:1, 0:1],
                                start=True, stop=True), reads=[b_mrow[pi], b_ones], writes=[pb[bkc]])
                    else:
                        half = 0.5 if k in (0, 2) else 1.0
                        S.dma("sp", prow[pi][:], postg_d[0:1, k * D + j * 512:k * D + (j + 1) * 512],
                              writes=[b_prow[pi]])
                        S.op("pool", lambda e, pi=pi: e.tensor_tensor(
                            grow[pi][0:1, :], mrow[pi][0:1, :], prow[pi][0:1, :], ALU.mult),
                            reads=[b_mrow[pi], b_prow[pi]], writes=[b_grow[pi]])
                        S.op("pool", lambda e, pi=pi, half=half: e.tensor_scalar(
                            grow[pi][0:1, :], grow[pi][0:1, :], half, None, ALU.mult),
                            reads=[b_grow[pi]], writes=[b_grow[pi]])
                        if k == 0:
                            bk2 = rot6()
                            S.op("pe", lambda e, bk2=bk2, pi=pi: e.matmul(
                                pa[bk2][:, :], ones_f[0:1, :], grow[pi][0:1, :],
                                start=True, stop=True), reads=[b_grow[pi], b_ones], writes=[pb[bk2]])
                            S.op("act", lambda e, j=j, bk2=bk2: e.copy(
                                gvecA[:, j * 512:(j + 1) * 512], pa[bk2][:, :]),
                                reads=[pb[bk2]], writes=[b_gA])
                        else:
                            S.dma("sp", (g2row_d if k == 1 else g3row_d)[0:1, j * 512:(j + 1) * 512],
                                  grow[pi][0:1, :], reads=[b_grow[pi]])
                    if j < 3:
                        yield
                if kind == 0:
                    S.op("act", lambda e, k=k: e.copy(
                        modcols[:, (2 * k + 1) * KC:(2 * k + 2) * KC], pa[bkc][:, 0:KC]),
                        reads=[pb[bkc]], writes=[b_mod])
                elif kind == 1:
                    S.op("act", lambda e, k=k: e.activation(
                        tmpc[:, :], pa[bkc][:, 0:KC], AF.Identity, bias=1.0, scale=1.0),
                        reads=[pb[bkc]], writes=[b_tmpc])
                    S.op("pool", lambda e, k=k: e.tensor_tensor(
                        modcols[:, (2 * k) * KC:(2 * k + 1) * KC], tmpc[:, :],
                        pregc[:, k * KC:(k + 1) * KC], ALU.mult),
                        reads=[b_tmpc, b_pregc], writes=[b_mod])
                yield

        with ExitStack() as p0:
            for _ in mod_vectors(list(range(NVA)), p0):
                pass
            S.barrier()
        b_ident.const = True
        b_ones.const = True

        if stop_after == "p0":
            dbg_mod = nc.dram_tensor("dbg_mod", [128, 6 * KC], F32, kind="ExternalOutput").ap()
            dbg_g = nc.dram_tensor("dbg_g", [128, D], F32, kind="ExternalOutput").ap()
            S.dma("sp", dbg_mod[:, :], modcols[:], reads=[b_mod])
            S.dma("sp", dbg_g[:, :], gvecA[:], reads=[b_gA])
            S.barrier()
            return nc


        def mcol(k, which, kc):
            c0 = (2 * k + which) * KC + kc
            return modcols[:, c0:c0 + 1]

        def prenorm_A_g(xs_, xb_, s, W):
            S.op("act", lambda e: e.activation(W["xn"][s % 2][:], xs_[:], AF.Square,
                                               accum_out=W["pss"][:, s:s + 1]),
                 reads=[xb_], writes=[W["pssb"][s], W["xnb"][s % 2]])
            yield
            S.op("act", lambda e: e.activation(W["pss"][:, s:s + 1], W["pss"][:, s:s + 1], AF.Sqrt,
                                               bias=EPS, scale=1.0 / D),
                 reads=[W["pssb"][s]], writes=[W["pssb"][s]])
            yield
            S.op("dve", lambda e: e.reciprocal(W["pss"][:, s:s + 1], W["pss"][:, s:s + 1]),
                 reads=[W["pssb"][s]], writes=[W["pssb"][s]])
            yield
            xn = W["xn"][s % 2]
            xnb = W["xnb"][s % 2]
            S.op("dve", lambda e: e.tensor_scalar(xn[:], xs_[:], W["pss"][:, s:s + 1], None, ALU.mult),
                 reads=[xb_, W["pssb"][s]], writes=[xnb])
            yield

        def prenorm_A(xs_, xb_, s, W):
            for _ in prenorm_A_g(xs_, xb_, s, W):
                pass

        def prenorm(xs_, xb_, k, s, hT, hTb, W):
            prenorm_A(xs_, xb_, s, W)
            prenorm_B(k, s, hT, hTb, W)

        def prenorm_B(k, s, hT, hTb, W):
            xn = W["xn"][s % 2]
            xnb = W["xnb"][s % 2]
            for g4 in range(4):
                bk = rot2()
                ptb = pa[bk][:].bitcast(BF16)
                for j in range(4):
                    kc = g4 * 4 + j
                    S.op("pe", lambda e, j=j, kc=kc: e.transpose(
                        ptb[:, j * 128:(j + 1) * 128], xn[:, kc * 128:(kc + 1) * 128], ident[:]),
                        reads=[xnb, b_ident], writes=[pb[bk]])
                for j in range(4):
                    kc = g4 * 4 + j
                    dst = hT[:, kc, s * 128:(s + 1) * 128]
                    if g4 % 2 == 0:
                        S.op("dve", lambda e, j=j, kc=kc, dst=dst: e.tensor_scalar(
                            dst, ptb[:, j * 128:(j + 1) * 128], mcol(k, 0, kc), mcol(k, 1, kc),
                            ALU.mult, ALU.add), reads=[pb[bk], b_mod], writes=[hTb[s]])
                    else:
                        S.op("act", lambda e, j=j, kc=kc, dst=dst: e.activation(
                            dst, ptb[:, j * 128:(j + 1) * 128], AF.Identity,
                            bias=mcol(k, 1, kc), scale=mcol(k, 0, kc)),
                            reads=[pb[bk], b_mod], writes=[hTb[s]])

        def ffn_gateup(hT, hTb, W, hook=None):
            for u in range(FC // 2):
                if hook is not None and u >= 1:
                    hook(u - 1)
                slot, sbf = ring.next("gu")
                wg = slot[:, 0:4096].rearrange("p (a b) -> p a b", a=KC)
                wu = slot[:, 4096:8192].rearrange("p (a b) -> p a b", a=KC)
                for mm in range(2):
                    m = 2 * u + mm
                    ig = rot6()
                    iu = rot6()
                    for kc in range(KC):
                        S.op("pe", lambda e, kc=kc: e.matmul(
                            pa[ig][:, :], wg[:, kc, mm * 128:(mm + 1) * 128], hT[:, kc, :],
                            start=(kc == 0), stop=(kc == KC - 1)),
                            reads=[sbf] + hTb, writes=[pb[ig]], inc=(kc == KC - 1))
                    for kc in range(KC):
                        S.op("pe", lambda e, kc=kc: e.matmul(
                            pa[iu][:, :], wu[:, kc, mm * 128:(mm + 1) * 128], hT[:, kc, :],
                            start=(kc == 0), stop=(kc == KC - 1)),
                            reads=[sbf] + hTb, writes=[pb[iu]], inc=(kc == KC - 1))
                    sg = W["sgt"][m % 2]
                    sgb = W["sgb"][m % 2]
                    S.op("act", lambda e: e.activation(sg[:], pa[ig][:, :], AF.Silu),
                         reads=[pb[ig]], writes=[sgb])
                    S.op("dve", lambda e: e.tensor_tensor(W["actT"][:, m, :], sg[:], pa[iu][:, :], ALU.mult),
                         reads=[sgb, pb[iu]], writes=[W["actb"][m]])

        def alloc_ffn_work(W, stk, tg):
            hT_t = sb("hT" + tg, [128, KC * T], BF16, stk)
            hT = hT_t[:, :].rearrange("p (a b) -> p a b", a=KC)
            hTb = S.bufs(NSUB, "hT")
            actT_t = sb("actT" + tg, [128, FC * T], BF16, stk)
            W["actT"] = actT_t[:, :].rearrange("p (a b) -> p a b", a=FC)
            W["actb"] = S.bufs(FC, "act")
            W["f_sb"] = [sb("f_sb%s%d" % (tg, i), [128, D], F32, stk) for i in range(NSUB)]
            W["fb"] = S.bufs(NSUB, "f")
            xst = [sb("xst%s%d" % (tg, i), [128, D], F32, stk) for i in range(2)]
            xstb = S.bufs(2, "xst")
            W["xn"] = [sb("xn%s%d" % (tg, i), [128, D], BF16, stk) for i in range(2)]
            W["xnb"] = S.bufs(2, "xn")
            W["sgt"] = [sb("sgt%s%d" % (tg, i), [128, T], F32, stk) for i in range(2)]
            W["sgb"] = S.bufs(2, "sg")
            W["pss"] = sb("pss" + tg, [128, 8], F32, stk)
            W["pssb"] = S.bufs(8, "pss")
            W["prs"] = sb("prs" + tg, [128, 8], F32, stk)
            W["prsb"] = S.bufs(8, "prs")
            return hT, hTb, xst, xstb

        def proj_down(W, lhs, lhsb, nch, units_tag, groups, after_sub=None):
            def evac(s, n, bk):
                if s % 2 == 0:
                    S.op("dve", lambda e: e.tensor_copy(
                        W["f_sb"][s][:, n * 512:(n + 1) * 512], pa[bk][:, :]),
                        reads=[pb[bk]], writes=[W["fb"][s]])
                else:
                    S.op("act", lambda e: e.copy(
                        W["f_sb"][s][:, n * 512:(n + 1) * 512], pa[bk][:, :]),
                        reads=[pb[bk]], writes=[W["fb"][s]])

            for n in range(4):
                banks = [rot6() for _ in range(NSUB)]
                if n < 3:
                    for (f0, f1) in groups:
                        slot, sbf = ring.next(units_tag)
                        wd = slot[:, 0:(f1 - f0) * 512].rearrange("p (a b) -> p a b", b=512)
                        for s in range(NSUB):
                            for fc in range(f0, f1):
                                S.op("pe", lambda e, s=s, fc=fc: e.matmul(
                                    pa[banks[s]][:, :], lhs[:, fc, s * 128:(s + 1) * 128], wd[:, fc - f0, :],
                                    start=(fc == 0), stop=(fc == nch - 1)),
                                    reads=[lhsb[fc], sbf], writes=[pb[banks[s]]], inc=(fc == f1 - 1))
                    for s in range(NSUB):
                        evac(s, n, banks[s])
                else:
                    wds = []
                    for gi, (f0, f1) in enumerate(groups):
                        slot, sbf = ring.next(units_tag, hold=gi)
                        wds.append((slot[:, 0:(f1 - f0) * 512].rearrange("p (a b) -> p a b", b=512), sbf, f0, f1))
                    for s in range(NSUB):
                        for (wd, sbf, f0, f1) in wds:
                            for fc in range(f0, f1):
                                S.op("pe", lambda e, s=s, fc=fc, wd=wd, f0=f0: e.matmul(
                                    pa[banks[s]][:, :], lhs[:, fc, s * 128:(s + 1) * 128], wd[:, fc - f0, :],
                                    start=(fc == 0), stop=(fc == nch - 1)),
                                    reads=[lhsb[fc], sbf], writes=[pb[banks[s]]], inc=(fc == f1 - 1))
                        evac(s, n, banks[s])
                        if after_sub is not None:
                            epiA, epiB, epiX = after_sub
                            if s == 1:
                                run_interleaved([epiA(0), epiA(1)])
                            elif s == 3:
                                epiB(0)
                                epiB(1)
                                run_interleaved([epiA(2), epiA(3)])
                                epiX(0)
                                epiX(1)
                                epiB(2)
                                epiB(3)
                                epiX(2)
                                epiX(3)

        def ffn_down(W, after_sub=None):
            proj_down(W, W["actT"], W["actb"], FC, "dn", FGROUPS, after_sub)

        def postnorm_residual_g(s, W, gvec, gvb, xs_, xb_):
            r = W["prs"][:, s:s + 1]
            rb = W["prsb"][s]
            S.op("act", lambda e: e.activation(W["xn"][s % 2][:], W["f_sb"][s][:], AF.Square, accum_out=r),
                 reads=[W["fb"][s]], writes=[rb, W["xnb"][s % 2]])
            yield
            S.op("act", lambda e: e.activation(r, r, AF.Sqrt, bias=EPS, scale=1.0 / D), reads=[rb], writes=[rb])
            yield
            S.op("dve", lambda e: e.reciprocal(r, r), reads=[rb], writes=[rb])
            yield
            f = W["f_sb"][s]
            S.op("dve", lambda e: e.scalar_tensor_tensor(f[:], f[:], r, gvec[:], ALU.mult, ALU.mult),
                 reads=[W["fb"][s], rb, gvb], writes=[W["fb"][s]])
            yield
            S.op("dve", lambda e: e.tensor_tensor(xs_[:], xs_[:], f[:], ALU.add),
                 reads=[xb_, W["fb"][s]], writes=[xb_])
            yield

        def postnorm_residual(s, W, gvec, gvb, xs_, xb_):
            for _ in postnorm_residual_g(s, W, gvec, gvb, xs_, xb_):
                pass

        def run_interleaved(gens):
            gens = list(gens)
            while gens:
                for g in list(gens):
                    try:
                        next(g)
                    except StopIteration:
                        gens.remove(g)

        with ExitStack() as p1:
            W = {}
            print("sbuf remaining at P1 start", nc.sbuf_bytes_remaining)
            hT, hTb, xst, xstb = alloc_ffn_work(W, p1, "a")
            cosT = sb("cosT", [128, T], F32, p1)
            sinT = sb("sinT", [128, T], F32, p1)
            b_cos, b_sin = S.bufs(2, "cs")
            tmpA, tmpB = W["sgt"]
            b_tA, b_tB = W["sgb"]
            print("sbuf remaining before staging", nc.sbuf_bytes_remaining)
            qst = [sb("qst%d" % i, [128, T], BF16, p1) for i in range(2)]
            qstb = S.bufs(2, "qst")
            vst = [sb("vst%d" % i, [128, T], BF16, p1) for i in range(2)]
            vstb = S.bufs(2, "vst")
            ust = [sb("ust%d" % i, [128, T], BF16, p1) for i in range(2)]
            ustb = S.bufs(2, "ust")
            zt = sb("zt", [128, UPAD], BF16, p1)
            b_zt = S.buf("zt")
            S.op("dve", lambda e: e.memset(zt[:], 0.0), writes=[b_zt])
            for cg in range(8):
                S.dma("sp", u_d[cg * 128:(cg + 1) * 128, 0:UPAD], zt[:], reads=[b_zt])
            zb = W["xn"][0]
            S.op("dve", lambda e: e.memset(zb[:, 0:KPAD], 0.0), writes=[W["xnb"][0]])
            for hh in range(8):
                S.dma("sp", kT_d[hh * 128:(hh + 1) * 128, 0:KPAD], zb[:, 0:KPAD], reads=[W["xnb"][0]])
                S.dma("sp", v_d[hh * 128:(hh + 1) * 128, :], zb[:, 0:1024], reads=[W["xnb"][0]])

            cnt = {"q": 0, "v": 0, "u": 0}
            for i in range(n_tiles):
                r0 = i * T
                own = i < NT_OWN
                for s in range(NSUB):
                    if i == 0 or s >= 2:
                        xs_, xb_ = xst[s % 2], xstb[s % 2]
                        S.dma("sp", xs_[:], x_d[r0 + s * 128:r0 + (s + 1) * 128, :], writes=[xb_])
                        prenorm_A(xs_, xb_, s, W)
                    prenorm_B(0, s, hT, hTb, W)
                if i == 0:
                    f2, f3 = W["f_sb"][2], W["f_sb"][3]
                    rows = [f2[0:1, q * 512:(q + 1) * 512] for q in range(4)] + \
                           [f3[0:1, q * 512:(q + 1) * 512] for q in range(4)]
                    vb = []
                    modgen1 = mod_vectors(list(range(NVA, NV0)), p1,
                                          views=(rows, W["f_sb"][1][:, 0:KC]), vbufs=vb)
                    ffn_gateup(hT, hTb, W, hook=lambda q: next(modgen1, None) if q < N_INTER else None)
                    for _ in modgen1:
                        pass
                    S.op("dve", lambda e: e.memset(W["f_sb"][1][:, 0:1], 0.0),
                         writes=vb + [W["fb"][1], W["fb"][2], W["fb"][3]])
                else:
                    ffn_gateup(hT, hTb, W)

                def epi1(s, r0=r0, own=own):
                    xs_, xb_ = xst[s % 2], xstb[s % 2]
                    S.dma("sp", xs_[:], x_d[r0 + s * 128:r0 + (s + 1) * 128, :], writes=[xb_])
                    yield
                    yield from postnorm_residual_g(s, W, gvecA, b_gA, xs_, xb_)
                    if own:
                        S.dma("sp", x1_d[r0 + s * 128:r0 + (s + 1) * 128, :], xs_[:], reads=[xb_])
                    yield from prenorm_A_g(xs_, xb_, s, W)

                vunits = []

                def vproj(s, r0=r0):
                    if not vunits:
                        for hq in range(2):
                            slot, sbf = ring.next("win", hold=hq)
                            vunits.append((slot[:, :].rearrange("p (a b) -> p a b", a=KC), sbf))
                    for jv, (wvv, sbfv) in enumerate(vunits):
                        bk = rot6()
                        for kc in range(KC):
                            S.op("pe", lambda e, kc=kc: e.matmul(
                                pa[bk][:, :], hT[:, kc, s * 128:(s + 1) * 128], wvv[:, kc, :],
                                start=(kc == 0), stop=(kc == KC - 1)),
                                reads=[sbfv, hTb[s]], writes=[pb[bk]])
                        vi = cnt["v"] % 2
                        cnt["v"] += 1
                        S.op("act", lambda e: e.copy(vst[vi][:], pa[bk][:, :]), reads=[pb[bk]], writes=[vstb[vi]])
                        S.dma("sp", v_d[KPAD + r0 + s * 128:KPAD + r0 + (s + 1) * 128, jv * 512:(jv + 1) * 512],
                              vst[vi][:], reads=[vstb[vi]])

                ffn_down(W, (epi1, lambda s: prenorm_B(1, s, hT, hTb, W), vproj))
                if i + 1 < n_tiles:
                    for s in range(2):
                        xs_, xb_ = xst[s % 2], xstb[s % 2]
                        S.dma("sp", xs_[:], x_d[r0 + T + s * 128:r0 + T + (s + 1) * 128, :], writes=[xb_])
                        prenorm_A(xs_, xb_, s, W)
                if dbg.get("p1_stop") == "c2":
                    break
                S.dma("sp", cosT[:], cos_d[:, r0:r0 + T], writes=[b_cos])
                S.dma("sp", sinT[:], sin_d[:, r0:r0 + T], writes=[b_sin])
                for j in (range(0, 4) if own else range(2, 4)):
                    slot, sbf = ring.next("win")
                    wv = slot[:, :].rearrange("p (a b) -> p a b", a=KC)
                    if j < 4:
                        for hh in range(4):
                            head = (j % 2) * 4 + hh
                            bk = rot6()
                            for kc in range(KC):
                                S.op("pe", lambda e, kc=kc: e.matmul(
                                    pa[bk][:, :], wv[:, kc, hh * 128:(hh + 1) * 128], hT[:, kc, :],
                                    start=(kc == 0), stop=(kc == KC - 1)),
                                    reads=[sbf] + hTb, writes=[pb[bk]])
                            S.op("dve", lambda e: e.tensor_tensor(tmpA[:], pa[bk][:, :], cosT[:], ALU.mult),
                                 reads=[pb[bk], b_cos], writes=[b_tA])
                            S.op("dve", lambda e: e.tensor_tensor(tmpB[0:64, :], pa[bk][64:128, :], sinT[0:64, :], ALU.mult),
                                 reads=[pb[bk], b_sin], writes=[b_tB])
                            S.op("dve", lambda e: e.tensor_tensor(tmpB[64:128, :], pa[bk][0:64, :], sinT[64:128, :], ALU.mult),
                                 reads=[pb[bk], b_sin], writes=[b_tB])
                            qi = cnt["q"] % 2
                            cnt["q"] += 1
                            S.op("dve", lambda e: e.tensor_tensor(qst[qi][:], tmpA[:], tmpB[:], ALU.add),
                                 reads=[b_tA, b_tB], writes=[qstb[qi]])
                            if j < 2:
                                dst = qT_d[head * 128:(head + 1) * 128, r0:r0 + T]
                            else:
                                dst = kT_d[head * 128:(head + 1) * 128, KPAD + r0:KPAD + r0 + T]
                            S.dma("sp", dst, qst[qi][:], reads=[qstb[qi]])
                    else:
                        for s in range(NSUB):
                            bk = rot6()
                            for kc in range(KC):
                                S.op("pe", lambda e, kc=kc: e.matmul(
                                    pa[bk][:, :], hT[:, kc, s * 128:(s + 1) * 128], wv[:, kc, :],
                                    start=(kc == 0), stop=(kc == KC - 1)),
                                    reads=[sbf, hTb[s]], writes=[pb[bk]])
                            vi = cnt["v"] % 2
                            cnt["v"] += 1
                            S.op("act", lambda e: e.copy(vst[vi][:], pa[bk][:, :]), reads=[pb[bk]], writes=[vstb[vi]])
                            S.dma("sp", v_d[KPAD + r0 + s * 128:KPAD + r0 + (s + 1) * 128, (j - 4) * 512:(j - 3) * 512],
                                  vst[vi][:], reads=[vstb[vi]])
                if i <= NT_OWN:
                    nt_c = T if own else 128
                    for j in range(4):
                        slot, sbf = ring.next("cv")
                        wcv = slot[:, 0:4096].rearrange("p (a b) -> p a b", a=KC)
                        wcg = slot[:, 4096:8192].rearrange("p (a b) -> p a b", a=KC)
                        for c2 in range(2):
                            cg = 2 * j + c2
                            iv = rot6()
                            ig = rot6()
                            for kc in range(KC):
                                S.op("pe", lambda e, kc=kc: e.matmul(
                                    pa[iv][:, 0:nt_c], wcv[:, kc, c2 * 128:(c2 + 1) * 128], hT[:, kc, 0:nt_c],
                                    start=(kc == 0), stop=(kc == KC - 1)),
                                    reads=[sbf] + hTb, writes=[pb[iv]])
                            for kc in range(KC):
                                S.op("pe", lambda e, kc=kc: e.matmul(
                                    pa[ig][:, 0:nt_c], wcg[:, kc, c2 * 128:(c2 + 1) * 128], hT[:, kc, 0:nt_c],
                                    start=(kc == 0), stop=(kc == KC - 1)),
                                    reads=[sbf] + hTb, writes=[pb[ig]])
                            sg = W["sgt"][cg % 2]
                            sgb = W["sgb"][cg % 2]
                            S.op("act", lambda e: e.activation(sg[:, 0:nt_c], pa[ig][:, 0:nt_c], AF.Sigmoid),
                                 reads=[pb[ig]], writes=[sgb])
                            ui = cnt["u"] % 2
                            cnt["u"] += 1
                            S.op("dve", lambda e: e.tensor_tensor(ust[ui][:, 0:nt_c], sg[:, 0:nt_c], pa[iv][:, 0:nt_c], ALU.mult),
                                 reads=[sgb, pb[iv]], writes=[ustb[ui]])
                            S.dma("sp", u_d[cg * 128:(cg + 1) * 128, UPAD + r0:UPAD + r0 + nt_c], ust[ui][:, 0:nt_c],
                                  reads=[ustb[ui]])
            S.barrier()

        if stop_after == "p1":
            return nc

        def rms_merge(allv, allb, gcol, row_base, sq, sqb, rstd, b_rstd, mg, mgb):
            for g in range(8):
                S.op("act", lambda e, g=g: e.activation(sq[g % 2], allv[:, g, :], AF.Square),
                     reads=[allb[g]], writes=[sqb[g % 2]])
                for tb in range(4):
                    S.op("pe", lambda e, g=g, tb=tb: e.matmul(
                        pa[tb][:, :], ones_f[:, :], sq[g % 2][:, tb * 512:(tb + 1) * 512],
                        start=(g == 0), stop=(g == 7)), reads=[sqb[g % 2], b_ones], writes=[pb[tb]])
            for tb in range(4):
                S.op("act", lambda e, tb=tb: e.activation(
                    rstd[:, tb * 512:(tb + 1) * 512], pa[tb][:, :], AF.Sqrt, bias=EPS, scale=1.0 / 1024),
                    reads=[pb[tb]], writes=[b_rstd])
            S.op("dve", lambda e: e.reciprocal(rstd, rstd), reads=[b_rstd], writes=[b_rstd])
            for g in range(8):
                S.op("dve", lambda e, g=g: e.scalar_tensor_tensor(
                    mg[g % 2], allv[:, g, :], gcol(g), rstd, ALU.mult, ALU.mult),
                    reads=[allb[g], b_rstd], writes=[mgb[g % 2]])
                S.dma("sp", merged_d[(row_base + g) * 128:(row_base + g + 1) * 128, :], mg[g % 2],
                      reads=[mgb[g % 2]])

        cpar = sb("cpar", [128, 40])
        b_cpar = S.buf("cpar")
        S.dma("sp", cpar[:], cpar_d[:, :], writes=[b_cpar])

        with ExitStack() as p3:
            call = sb("call", [128, 8 * TOK_OWN], F32, p3)
            callv = call[:, :].rearrange("p (g t) -> p g t", g=8)
            callb = S.bufs(8, "call")
            utb16 = [sb("utb%d" % i, [128, TOK_OWN + 32], BF16, p3) for i in range(2)]
            utbb = S.bufs(2, "utb")
            ut0 = sb("ut0", [128, TOK_OWN], F32, p3)
            ut = [ut0, ut0]
            ut0b = S.buf("ut")
            utb = [ut0b, ut0b]
            dg = [sb("dg%d" % i, [128, 31 * 128], BF16, p3) for i in range(2)]
            dgb = S.bufs(2, "dg")
            cw = sb("cw", [128, 8 * 31], F32, p3)
            b_cw = S.buf("cw")
            S.dma("sp", cw[:], convw_d[:, :], writes=[b_cw])
            modgen = mod_vectors(list(range(NV0, 9)), p3)
            for cg in range(8):
                u_, ub_ = utb16[cg % 2], utbb[cg % 2]
                S.dma("sp", u_[:], u_d[cg * 128:(cg + 1) * 128, 0:TOK_OWN + 32], writes=[ub_])
                dg_, dgb_ = dg[cg % 2], dgb[cg % 2]
                for j in range(31):
                    wj = cw[:, cg * 31 + j:cg * 31 + j + 1]
                    S.op("dve", lambda e, wj=wj, j=j: e.tensor_scalar(
                        dg_[:, j * 128:(j + 1) * 128], ident[:, :], wj, None, ALU.mult),
                        reads=[b_ident, b_cw], writes=[dgb_], selfsync=False)
                for tb in range(4):
                    bk = rot6()
                    for j in range(31):
                        S.op("pe", lambda e, j=j, tb=tb: e.matmul(
                            pa[bk][:, :], dg_[:, j * 128:(j + 1) * 128],
                            u_[:, 1 + j + tb * 512:1 + j + (tb + 1) * 512],
                            start=(j == 0), stop=(j == 30)), reads=[dgb_, ub_], writes=[pb[bk]])
                    dst = callv[:, cg, tb * 512:(tb + 1) * 512]
                    if tb % 2 == 0:
                        S.op("act", lambda e, dst=dst: e.activation(
                            dst, pa[bk][:, :], AF.Identity, bias=cpar[:, cg:cg + 1], scale=1.0),
                            reads=[pb[bk], b_cpar], writes=[callb[cg]])
                    else:
                        S.op("dve", lambda e, dst=dst: e.tensor_scalar(
                            dst, pa[bk][:, :], cpar[:, cg:cg + 1], None, ALU.add),
                            reads=[pb[bk], b_cpar], writes=[callb[cg]])
                for _ in range(2):
                    next(modgen, None)
            for _ in modgen:
                pass
            sqa = [ut[i][:, :] for i in range(2)]
            sqab = utb
            mean = sb("mean", [128, TOK_OWN], F32, p3)
            b_mean = S.buf("mean")
            lrs = sb("lrs", [128, TOK_OWN], F32, p3)
            b_lrs = S.buf("lrs")
            for cg in range(8):
                for tb in range(4):
                    S.op("pe", lambda e, cg=cg, tb=tb: e.matmul(
                        pa[tb][:, :], ones_f[:, :], callv[:, cg, tb * 512:(tb + 1) * 512],
                        start=(cg == 0), stop=(cg == 7)), reads=[callb[cg], b_ones], writes=[pb[tb]])
            for tb in range(4):
                S.op("act", lambda e, tb=tb: e.activation(
                    mean[:, tb * 512:(tb + 1) * 512], pa[tb][:, :], AF.Copy, scale=1.0 / 1024),
                    reads=[pb[tb]], writes=[b_mean])
            for cg in range(8):
                S.op("act", lambda e, cg=cg: e.activation(sqa[cg % 2], callv[:, cg, :], AF.Square),
                     reads=[callb[cg]], writes=[sqab[cg % 2]])
                for tb in range(4):
                    S.op("pe", lambda e, cg=cg, tb=tb: e.matmul(
                        pa[tb][:, :], ones_f[:, :], sqa[cg % 2][:, tb * 512:(tb + 1) * 512],
                        start=(cg == 0), stop=(cg == 7)), reads=[sqab[cg % 2], b_ones], writes=[pb[tb]])
            S.op("dve", lambda e: e.tensor_tensor(lrs[:], mean[:], mean[:], ALU.mult),
                 reads=[b_mean], writes=[b_lrs])
            for tb in range(4):
                S.op("dve", lambda e, tb=tb: e.scalar_tensor_tensor(
                    lrs[:, tb * 512:(tb + 1) * 512], pa[tb][:, :], 1.0 / 1024, lrs[:, tb * 512:(tb + 1) * 512],
                    ALU.mult, ALU.subtract), reads=[pb[tb], b_lrs], writes=[b_lrs])
            S.op("act", lambda e: e.activation(lrs[:], lrs[:], AF.Sqrt, bias=EPS, scale=1.0),
                 reads=[b_lrs], writes=[b_lrs])
            S.op("dve", lambda e: e.reciprocal(lrs[:], lrs[:]), reads=[b_lrs], writes=[b_lrs])
            for cg in range(8):
                acc = callv[:, cg, :]
                S.op("dve", lambda e, acc=acc: e.tensor_tensor(acc, acc, mean[:], ALU.subtract),
                     reads=[b_mean], writes=[callb[cg]])
                S.op("dve", lambda e, acc=acc: e.tensor_tensor(acc, acc, lrs[:], ALU.mult),
                     reads=[b_lrs], writes=[callb[cg]])
                S.op("act", lambda e, acc=acc, cg=cg: e.activation(
                    acc, acc, AF.Silu, bias=cpar[:, 16 + cg:16 + cg + 1], scale=cpar[:, 8 + cg:8 + cg + 1]),
                    reads=[b_cpar], writes=[callb[cg]])
            rms_merge(callv, callb, lambda g: cpar[:, 24 + g:24 + g + 1], 8, sqa, sqab, lrs[:, :], b_lrs,
                      [dg[0][:, 0:TOK_OWN], dg[1][:, 0:TOK_OWN]], dgb)
            S.barrier()

        if stop_after == "p3":
            return nc

        with ExitStack() as p2:
            aall = sb("aall", [128, 8 * TOK_OWN], F32, p2)
            aallv = aall[:, :].rearrange("p (g t) -> p g t", g=8)
            aallb = S.bufs(8, "aall")
            qh = sb("qh", [128, TOK_OWN], BF16, p2)
            kh = sb("kh", [128, KPAD + TOK_ALL], BF16, p2)
            b_qh, b_kh = S.bufs(2, "qk")
            NVT = 69
            vt = [sb("vt%d" % i, [128, NVT * 128], BF16, p2) for i in range(2)]
            vtb = S.bufs(2, "vt")
            accn = sb("accn", [128, TOK_OWN], F32, p2)
            accd = sb("accd", [128, TOK_OWN], F32, p2)
            b_an, b_ad = S.bufs(2, "acc")
            ptl = [sb("pt%d" % i, [128, 256], BF16, p2) for i in range(3)]
            ptb_ = S.bufs(3, "pt")
            maskb = sb("maskb", [128, 512], BF16, p2)
            b_mask = S.buf("mask")
            ones_b = sb("ones_b", [128, 128], BF16, p2)
            b_onesb = S.buf("onesb")
            S.dma("sp", maskb[:], maskb_d[:, :], writes=[b_mask])
            S.op("dve", lambda e: e.memset(ones_b[:], 1.0), writes=[b_onesb])
            b_mask.const = True
            b_onesb.const = True
            PATS = ((1, 16), (4, 4), (16, 1))
            nchunk = 0
            vbase = {}

            def load_v(h):
                vth, vthb = vt[h % 2], vtb[h % 2]
                vidx = 0
                for (d, nblk) in PATS:
                    nch = nblk + 1
                    base = KPAD - 64 * d
                    rows = v_d[base:base + 128 * d * nch, h * 128:(h + 1) * 128].rearrange(
                        "(c a dd) e -> a c dd e", a=128, dd=d)
                    for r in range(d):
                        dst = vth[:, vidx * 128:(vidx + nch) * 128].rearrange("p (c e) -> p c e", c=nch)
                        S.dma("sp", dst, rows[:, :, r, :], writes=[vthb])
                        vbase[(d, r)] = vidx
                        vidx += nch

            load_v(0)
            for h in range(HEADS):
                S.dma("sp", qh[:], qT_d[h * 128:(h + 1) * 128, :], writes=[b_qh])
                S.dma("sp", kh[:], kT_d[h * 128:(h + 1) * 128, :], writes=[b_kh])
                if h + 1 < HEADS:
                    load_v(h + 1)
                vth, vthb = vt[h % 2], vtb[h % 2]
                chunks = []
                for pi, (d, nblk) in enumerate(PATS):
                    for r in range(d):
                        for c in range(nblk + 1):
                            chunks.append((pi, d, nblk, r, c))

                def emit_scores(ci):
                    pi, d, nblk, r, c = chunks[ci]
                    qv = qh[:, :].rearrange("p (m dd) -> p dd m", dd=d)
                    kv = kh[:, :].rearrange("p (m dd) -> p dd m", dd=d)
                    q0 = max(c - 1, 0) * 128
                    q1 = min(c + 1, nblk) * 128
                    ncol = q1 - q0
                    mc0 = (1 if c == 0 else 0) * 256 + (128 if c == 0 else 0)
                    kst = KPAD // d - 64 + 128 * c
                    g = nchunk0 + ci
                    bst = g % 2
                    pti = g % 3
                    S.op("pe", lambda e: e.matmul(
                        pa[bst][:, 0:ncol], kv[:, r, kst:kst + 128], qv[:, r, q0:q1],
                        start=True, stop=False), reads=[b_kh, b_qh], writes=[pb[bst]])
                    S.op("pe", lambda e: e.matmul(
                        pa[bst][:, 0:ncol], ident[:, :], maskb[:, mc0:mc0 + ncol],
                        start=False, stop=True), reads=[b_ident, b_mask], writes=[pb[bst]])
                    S.op("act", lambda e: e.activation(
                        ptl[pti][:, 0:ncol], pa[bst][:, 0:ncol], AF.Exp, scale=SCALE),
                        reads=[pb[bst]], writes=[ptb_[pti]])

                def emit_pv(ci):
                    pi, d, nblk, r, c = chunks[ci]
                    anv = accn[:, :].rearrange("p (m dd) -> p dd m", dd=d)
                    adv = accd[:, :].rearrange("p (m dd) -> p dd m", dd=d)
                    q0 = max(c - 1, 0) * 128
                    g = nchunk0 + ci
                    pti = g % 3
                    vtile = vth[:, (vbase[(d, r)] + c) * 128:(vbase[(d, r)] + c + 1) * 128]
                    for jb in (c - 1, c):
                        if jb < 0 or jb >= nblk:
                            continue
                        pc0 = jb * 128 - q0
                        first = (jb == c)
                        bn = 2 + jb % 2
                        bd = 4 + jb % 2
                        S.op("pe", lambda e: e.matmul(
                            pa[bn][:, 0:128], vtile, ptl[pti][:, pc0:pc0 + 128],
                            start=first, stop=not first), reads=[vthb, ptb_[pti]], writes=[pb[bn]])
                        S.op("pe", lambda e: e.matmul(
                            pa[bd][:, 0:128], ones_b[:, :], ptl[pti][:, pc0:pc0 + 128],
                            start=first, stop=not first), reads=[b_onesb, ptb_[pti]], writes=[pb[bd]])
                    if c >= 1:
                        jb = c - 1
                        bn = 2 + jb % 2
                        bd = 4 + jb % 2
                        dn_ = anv[:, r, jb * 128:(jb + 1) * 128]
                        dd_ = adv[:, r, jb * 128:(jb + 1) * 128]
                        if pi == 0:
                            S.op("act", lambda e: e.copy(dn_, pa[bn][:, 0:128]),
                                 reads=[pb[bn]], writes=[b_an], selfsync=False)
                            S.op("act", lambda e: e.copy(dd_, pa[bd][:, 0:128]),
                                 reads=[pb[bd]], writes=[b_ad], selfsync=False)
                        else:
                            S.op("dve", lambda e: e.tensor_tensor(dn_, dn_, pa[bn][:, 0:128], ALU.add),
                                 reads=[pb[bn]], writes=[b_an], selfsync=False)
                            S.op("dve", lambda e: e.tensor_tensor(dd_, dd_, pa[bd][:, 0:128], ALU.add),
                                 reads=[pb[bd]], writes=[b_ad], selfsync=False)

                nchunk0 = nchunk
                emit_scores(0)
                for ci in range(len(chunks)):
                    if ci + 1 < len(chunks):
                        emit_scores(ci + 1)
                    emit_pv(ci)
                nchunk += len(chunks)
                S.op("dve", lambda e: e.reciprocal(accd[:], accd[:]), reads=[b_ad], writes=[b_ad])
                S.op("dve", lambda e, h=h: e.tensor_tensor(aallv[:, h, :], accn[:], accd[:], ALU.mult),
                     reads=[b_an, b_ad], writes=[aallb[h]])
            rms_merge(aallv, aallb, lambda g: cpar[:, 32 + g:32 + g + 1], 0,
                      [accn[:, :], accd[:, :]], [b_an, b_ad],
                      vt[0][:, 0:2 * TOK_OWN].bitcast(F32), vtb[0],
                      [vt[1][:, 0:TOK_OWN], vt[1][:, TOK_OWN:2 * TOK_OWN]], [vtb[1], vtb[1]])
            S.barrier()

        if stop_after == "p2":
            return nc

        with ExitStack() as p4:
            W = {}
            hT, hTb, xst, xstb = alloc_ffn_work(W, p4, "b")
            gvecB = sb("gvecB", [128, D], F32, p4)
            b_gB = S.buf("gB")
            mtv = W["actT"][:, 0:KC, :]
            mtb = W["actb"][0:KC]
            x2b = S.bufs(NSUB, "x2d")
            S.dma("sp", gvecB[:], g2row_d[0:1, :].partition_broadcast(128), writes=[b_gB])
            S.dma("sp", gvecA[:], g3row_d[0:1, :].partition_broadcast(128), writes=[b_gA])
            merged_v = merged_d.rearrange("(c p) t -> p c t", p=128)
            for i in range(dbg.get("n_tiles4", NT_OWN)):
                r0 = i * T
                S.dma("sp", mtv, merged_v[:, :, r0:r0 + T], writes=mtb)

                def epi_w(s, r0=r0):
                    xs_, xb_ = xst[s % 2], xstb[s % 2]
                    S.dma("sp", xs_[:], x1_d[r0 + s * 128:r0 + (s + 1) * 128, :], writes=[xb_])
                    yield
                    yield from postnorm_residual_g(s, W, gvecB, b_gB, xs_, xb_)
                    S.dma("sp", x2_d[r0 + s * 128:r0 + (s + 1) * 128, :], xs_[:], reads=[xb_], writes=[x2b[s]])
                    yield from prenorm_A_g(xs_, xb_, s, W)

                proj_down(W, mtv, mtb, KC, "wout", ((0, KC),),
                          (epi_w, lambda s: prenorm_B(2, s, hT, hTb, W), lambda s: None))
                ffn_gateup(hT, hTb, W)

                def epi2(s, r0=r0):
                    xs_, xb_ = xst[s % 2], xstb[s % 2]
                    S.dma("sp", xs_[:], x2_d[r0 + s * 128:r0 + (s + 1) * 128, :], reads=[x2b[s]], writes=[xb_])
                    yield
                    yield from postnorm_residual_g(s, W, gvecA, b_gA, xs_, xb_)
                    S.dma("sp", out_d[r0 + s * 128:r0 + (s + 1) * 128, :], xs_[:], reads=[xb_])
                    yield

                ffn_down(W, (epi2, lambda s: None, lambda s: None))
            S.barrier()
    return nc


def _cols(v, n):
    return np.ascontiguousarray(np.asarray(v, np.float32).reshape(n, 128).T)


def prep_core_inputs(core, inp, shared):
    b = core // 2
    rev = core % 2 == 1
    x = inp["x"][b]
    S = x.shape[0]
    if not rev:
        gpos = np.arange(0, TOK_ALL)
    else:
        gpos = S - 1 - np.arange(0, TOK_ALL)
    m = dict(shared)
    m["x"] = np.ascontiguousarray(x[gpos])
    m["c_col"] = _cols(inp["c"][b], KC)
    cw = np.asarray(inp["conv_w"][0], np.float32)
    if rev:
        cw = cw[::-1]
    m["conv_w_cols"] = np.ascontiguousarray(
        cw.T.reshape(8, 128, 31).transpose(1, 0, 2).reshape(128, 8 * 31))
    half = 64
    inv_freq = (10000.0 ** (-np.arange(half, dtype=np.float32) / half)).astype(np.float32)
    ang = gpos.astype(np.float32)[None, :] * inv_freq[:, None]
    cos = np.cos(ang).astype(np.float32)
    sin = np.sin(ang).astype(np.float32)
    m["rope_cos"] = np.ascontiguousarray(np.concatenate([cos, cos], 0))
    m["rope_sin"] = np.ascontiguousarray(np.concatenate([-sin, sin], 0))
    return m


def prep_shared(inp):
    sh = {}
    for k in ("w_ada", "ffn1_w_gate", "ffn1_w_up", "ffn1_w_down", "ffn2_w_gate", "ffn2_w_up",
              "ffn2_w_down", "w_in", "w_out"):
        sh[k] = np.ascontiguousarray(np.asarray(inp[k], np.float32)[0])
    sh["b_ada"] = np.ascontiguousarray(np.asarray(inp["b_ada"], np.float32).reshape(1, -1))
    sh["pre_g_cols"] = np.ascontiguousarray(np.concatenate(
        [_cols(inp[k][0], KC) for k in ("ffn1_pre_g", "mix_pre_g", "ffn2_pre_g")], 1))
    sh["post_g_rows"] = np.ascontiguousarray(np.concatenate(
        [np.asarray(inp[k][0], np.float32) for k in ("ffn1_post_g", "mix_post_g", "ffn2_post_g")])[None, :])
    sh["cpar_cols"] = np.ascontiguousarray(np.concatenate(
        [_cols(inp[k][0], 8) for k in ("conv_b", "conv_ln_g", "conv_ln_b", "conv_out_g", "attn_out_g")], 1))
    sh["ident"] = np.eye(128, dtype=np.float32).astype(ml_dtypes.bfloat16)
    a = np.arange(128)[:, None]
    bq = np.arange(128)[None, :]
    NEG = -30000.0
    mB = np.where(a <= bq, 0.0, NEG)
    mA = np.where(a >= bq, 0.0, NEG)
    normal = np.concatenate([mB, mA], 1)
    edge = normal.copy()
    edge[:64, :] = NEG
    sh["maskb"] = np.ascontiguousarray(np.concatenate([normal, edge], 1).astype(np.float32)).astype(ml_dtypes.bfloat16)
    return sh


def kernel(**inputs):
    inp = {k: np.asarray(v) for k, v in inputs.items()}
    B, S, Dm = inp["x"].shape
    shared = prep_shared(inp)
    in_maps = [prep_core_inputs(c, inp, shared) for c in range(8)]
    nc = build_nc()
    res = run_bass_kernel_spmd(nc, in_maps, core_ids=list(range(8)))
    out = np.empty((B, S, Dm), np.float32)
    for c in range(8):
        o = res.results[c]["out"]
        b = c // 2
        if c % 2 == 0:
            out[b, 0:TOK_OWN] = o
        else:
            out[b, S - 1 - np.arange(TOK_OWN)] = o
    return out
```
